# Optimizing a Trainium2 kernel written in Bass

```python
import jax
import jax.numpy as jnp
from jax import lax
import numpy as np

D_MODEL = 1024
BATCH = 4
SEQ = 4096
DEPTH = 4

GRID_W = 64
CTX_LEN = 256
N_MIXERS = 4
EPS = 1e-6
N_MOD = 6

FNET_GROUPS = 4
FNET_GROUP_DIM = D_MODEL // FNET_GROUPS

NA_HEADS = 16
NA_HEAD_DIM = D_MODEL // NA_HEADS
NA_WIN_ROWS = 8
NA_WIN_COLS = 16

SG_CHUNK = 128
SG_GROUPS = 4
SG_DIM = D_MODEL
SG_GROUP_DIM = SG_DIM // SG_GROUPS

ATT_HEADS = 16
ATT_KV_HEADS = 4
ATT_GROUP = ATT_HEADS // ATT_KV_HEADS
ATT_HEAD_DIM = D_MODEL // ATT_HEADS
ATT_Q_BLOCK = 128
ROPE_THETA = 10000.0

FFN_DIM = 2816
FFN_CONV = 3

kernel_name = "hybrid_interleaved_diffusion_trunk"


def _layers_using(m):
    return (DEPTH - m + N_MIXERS - 1) // N_MIXERS


def rms_norm(x, g):
    xf = x.astype(jnp.float32)
    y = xf * lax.rsqrt(jnp.mean(xf * xf, axis=-1, keepdims=True) + EPS)
    return (y * g.astype(jnp.float32)).astype(x.dtype)


def modulate(h, g, shift, scale):
    return rms_norm(h, g) * (1 + scale) + shift


def ada_params(cond, w_mod, b_mod):
    m = jax.nn.silu(cond) @ w_mod + b_mod
    return jnp.split(m[:, None, :], N_MOD, axis=-1)


def depthwise_conv_seq(h, w, b):
    pad = FFN_CONV // 2
    y = lax.conv_general_dilated(h, w[:, None, :].astype(h.dtype), window_strides=(1,),
                                 padding=((pad, FFN_CONV - 1 - pad),),
                                 dimension_numbers=("NWC", "WIO", "NWC"),
                                 feature_group_count=h.shape[-1])
    return y + b


def conv_ffn(h, w_up, w_conv, b_conv, w_down):
    g, v = jnp.split(h @ w_up, 2, axis=-1)
    g = depthwise_conv_seq(g, w_conv, b_conv)
    return (jax.nn.silu(g) * v) @ w_down


def fourier_mix(a, w_out):
    B, L, D = a.shape
    ag = a.reshape(B, L, FNET_GROUPS, FNET_GROUP_DIM).astype(jnp.float32)
    f = jnp.fft.fft2(ag, axes=(1, 3), norm="ortho").real
    return f.astype(a.dtype).reshape(B, L, D) @ w_out


def spatial_gating_mix(a, w_in, g_v, w_s, b_s, w_out):
    B, L, _ = a.shape
    z = jax.nn.gelu(a @ w_in)
    u, v = jnp.split(z, 2, axis=-1)
    v = rms_norm(v, g_v)
    n = L // SG_CHUNK
    vg = v.reshape(B, n, SG_CHUNK, SG_GROUPS, SG_GROUP_DIM)
    mixed = jnp.einsum("gpq,bnqgc->bnpgc", w_s, vg) + b_s.T[None, None, :, :, None]
    return (u * mixed.reshape(B, L, SG_DIM)) @ w_out


def axial_rope(L):
    t = jnp.arange(L)
    row = (t // GRID_W).astype(jnp.float32)
    col = (t % GRID_W).astype(jnp.float32)
    n_freq = ATT_HEAD_DIM // 4
    inv_freq = ROPE_THETA ** (-jnp.arange(n_freq, dtype=jnp.float32) / n_freq)
    ang = jnp.concatenate([row[:, None] * inv_freq, col[:, None] * inv_freq], axis=-1)
    return jnp.cos(ang), jnp.sin(ang)


def apply_rope(x, cos, sin):
    xf = x.astype(jnp.float32)
    x1, x2 = jnp.split(xf, 2, axis=-1)
    c = cos[None, :, None, :]
    s = sin[None, :, None, :]
    return jnp.concatenate([x1 * c - x2 * s, x1 * s + x2 * c], axis=-1).astype(x.dtype)


def _gqa_attend(q, k, v):
    s = jnp.einsum("bqkgd,bskd->bkgqs", q, k).astype(jnp.float32) * (ATT_HEAD_DIM ** -0.5)
    p = jax.nn.softmax(s, axis=-1).astype(v.dtype)
    return jnp.einsum("bkgqs,bskd->bqkgd", p, v)


def gqa_mix(a_lat, a_ctx, w_qkv, g_q, g_k, w_out, need_ctx_out):
    B, L, _ = a_lat.shape
    H, KV, G, Dh = ATT_HEADS, ATT_KV_HEADS, ATT_GROUP, ATT_HEAD_DIM
    q, k, v = jnp.split(a_lat @ w_qkv, [H * Dh, (H + KV) * Dh], axis=-1)
    q = rms_norm(q.reshape(B, L, H, Dh), g_q)
    k = rms_norm(k.reshape(B, L, KV, Dh), g_k)
    v = v.reshape(B, L, KV, Dh)
    cos, sin = axial_rope(L)
    q = apply_rope(q, cos, sin)
    k = apply_rope(k, cos, sin)
    n_ctx = a_ctx.shape[1]
    kc, vc = jnp.split(a_ctx @ w_qkv[:, H * Dh:], 2, axis=-1)
    kc = rms_norm(kc.reshape(B, n_ctx, KV, Dh), g_k)
    vc = vc.reshape(B, n_ctx, KV, Dh)
    k_all = jnp.concatenate([k, kc], axis=1)
    v_all = jnp.concatenate([v, vc], axis=1)
    nblk = L // ATT_Q_BLOCK
    q_blocks = q.reshape(B, nblk, ATT_Q_BLOCK, KV, G, Dh).swapaxes(0, 1)
    o = lax.map(lambda qb: _gqa_attend(qb, k_all, v_all), q_blocks)
    y_lat = o.swapaxes(0, 1).reshape(B, L, H * Dh) @ w_out
    y_ctx = None
    if need_ctx_out:
        qc = rms_norm((a_ctx @ w_qkv[:, :H * Dh]).reshape(B, n_ctx, H, Dh), g_q)
        oc = _gqa_attend(qc.reshape(B, n_ctx, KV, G, Dh), kc, vc)
        y_ctx = oc.reshape(B, n_ctx, H * Dh) @ w_out
    return y_lat, y_ctx


def neighbourhood_mix(a_lat, a_ctx, w_qkv, rpb, w_out, need_ctx_out):
    B, L, D = a_lat.shape
    H, Dh = NA_HEADS, NA_HEAD_DIM
    rows = L // GRID_W
    kh = min(NA_WIN_ROWS, rows)
    kw = NA_WIN_COLS
    scale = Dh ** -0.5
    q, k, v = jnp.split(a_lat @ w_qkv, 3, axis=-1)
    n_ctx = a_ctx.shape[1]
    kc, vc = jnp.split(a_ctx @ w_qkv[:, D:], 2, axis=-1)
    kc = kc.reshape(B, n_ctx, H, Dh)
    vc = vc.reshape(B, n_ctx, H, Dh)
    k_grid = k.reshape(B, rows, GRID_W, H, Dh)
    v_grid = v.reshape(B, rows, GRID_W, H, Dh)
    col = jnp.arange(GRID_W)
    col_start = jnp.clip(col - kw // 2, 0, GRID_W - kw)
    col_idx = col_start[:, None] + jnp.arange(kw)[None, :]
    col_off = col_idx - col[:, None] + (NA_WIN_COLS - 1)
    q_rows = q.reshape(B, rows, GRID_W, H, Dh).swapaxes(0, 1)

    def row_block(args):
        r, q_r = args
        r_start = jnp.clip(r - kh // 2, 0, rows - kh)
        k_r = lax.dynamic_slice_in_dim(k_grid, r_start, kh, axis=1)[:, :, col_idx]
        v_r = lax.dynamic_slice_in_dim(v_grid, r_start, kh, axis=1)[:, :, col_idx]
        row_off = r_start + jnp.arange(kh) - r + (NA_WIN_ROWS - 1)
        bias = rpb[:, row_off[:, None, None], col_off[None, :, :]].astype(jnp.float32)
        s_nb = jnp.einsum("bwhd,bawjhd->bhwaj", q_r, k_r).astype(jnp.float32) * scale
        s_nb = (s_nb + bias.transpose(0, 2, 1, 3)[None]).reshape(B, H, GRID_W, kh * kw)
        s_ctx = jnp.einsum("bwhd,bchd->bhwc", q_r, kc).astype(jnp.float32) * scale
        p = jax.nn.softmax(jnp.concatenate([s_nb, s_ctx], axis=-1), axis=-1).astype(q_r.dtype)
        p_nb = p[..., :kh * kw].reshape(B, H, GRID_W, kh, kw)
        p_ctx = p[..., kh * kw:]
        return (jnp.einsum("bhwaj,bawjhd->bwhd", p_nb, v_r)
                + jnp.einsum("bhwc,bchd->bwhd", p_ctx, vc))

    o = lax.map(row_block, (jnp.arange(rows), q_rows))
    y_lat = o.swapaxes(0, 1).reshape(B, L, D) @ w_out
    y_ctx = None
    if need_ctx_out:
        qc = (a_ctx @ w_qkv[:, :D]).reshape(B, n_ctx, H, Dh)
        s = jnp.einsum("bqhd,bkhd->bhqk", qc, kc).astype(jnp.float32) * scale
        p = jax.nn.softmax(s, axis=-1).astype(vc.dtype)
        y_ctx = jnp.einsum("bhqk,bkhd->bqhd", p, vc).reshape(B, n_ctx, D) @ w_out
    return y_lat, y_ctx


def _normal(k, shape, scale):
    return jax.random.normal(k, shape, jnp.float32) * scale


def setup_inputs(seed: int = 0) -> dict:
    key = jax.random.key(seed)
    ks = iter(jax.random.split(key, 32))
    D = D_MODEL
    inv = D ** -0.5
    n_a, n_b, n_c, n_d = (_layers_using(m) for m in range(N_MIXERS))
    qkv_dim = (ATT_HEADS + 2 * ATT_KV_HEADS) * ATT_HEAD_DIM
    return {
        "x": _normal(next(ks), (BATCH, SEQ, D), 1.0),
        "c": _normal(next(ks), (BATCH, D), 1.0),
        "ctx": _normal(next(ks), (BATCH, CTX_LEN, D), 1.0),
        "c_ctx": _normal(next(ks), (D,), 1.0),
        "w_mod": _normal(next(ks), (DEPTH, D, N_MOD * D), 0.5 * inv),
        "b_mod": _normal(next(ks), (DEPTH, N_MOD * D), 0.01),
        "g_norm_mix": 1.0 + _normal(next(ks), (DEPTH, D), 0.02),
        "g_norm_ffn": 1.0 + _normal(next(ks), (DEPTH, D), 0.02),
        "w_ffn_up": _normal(next(ks), (DEPTH, D, 2 * FFN_DIM), inv),
        "w_ffn_conv": _normal(next(ks), (DEPTH, FFN_CONV, FFN_DIM), FFN_CONV ** -0.5),
        "b_ffn_conv": _normal(next(ks), (DEPTH, FFN_DIM), 0.01),
        "w_ffn_down": _normal(next(ks), (DEPTH, FFN_DIM, D), FFN_DIM ** -0.5),
        "w_fnet_out": _normal(next(ks), (n_a, D, D), inv),
        "w_na_qkv": _normal(next(ks), (n_b, D, 3 * D), inv),
        "na_rel_bias": _normal(next(ks), (n_b, NA_HEADS, 2 * NA_WIN_ROWS - 1, 2 * NA_WIN_COLS - 1), 0.1),
        "w_na_out": _normal(next(ks), (n_b, D, D), inv),
        "w_sg_in": _normal(next(ks), (n_c, D, 2 * SG_DIM), inv),
        "g_sg_v": 1.0 + _normal(next(ks), (n_c, SG_DIM), 0.02),
        "w_sg_spatial": _normal(next(ks), (n_c, SG_GROUPS, SG_CHUNK, SG_CHUNK), SG_CHUNK ** -0.5),
        "b_sg_spatial": 1.0 + _normal(next(ks), (n_c, SG_GROUPS, SG_CHUNK), 0.01),
        "w_sg_out": _normal(next(ks), (n_c, SG_DIM, D), SG_DIM ** -0.5),
        "w_att_qkv": _normal(next(ks), (n_d, D, qkv_dim), inv),
        "g_att_q": 1.0 + _normal(next(ks), (n_d, ATT_HEAD_DIM), 0.02),
        "g_att_k": 1.0 + _normal(next(ks), (n_d, ATT_HEAD_DIM), 0.02),
        "w_att_out": _normal(next(ks), (n_d, ATT_HEADS * ATT_HEAD_DIM, D), (ATT_HEADS * ATT_HEAD_DIM) ** -0.5),
        "g_final": 1.0 + _normal(next(ks), (D,), 0.02),
    }


def reference(x, c, ctx, c_ctx, w_mod, b_mod, g_norm_mix, g_norm_ffn, w_ffn_up, w_ffn_conv, b_ffn_conv,
              w_ffn_down, w_fnet_out, w_na_qkv, na_rel_bias, w_na_out, w_sg_in, g_sg_v, w_sg_spatial,
              b_sg_spatial, w_sg_out, w_att_qkv, g_att_q, g_att_k, w_att_out, g_final):
    h_lat, h_ctx = x, ctx
    for i in range(DEPTH):
        m, j = i % N_MIXERS, i // N_MIXERS
        last = i == DEPTH - 1
        ctx_in_needed = (not last) or m in (1, 3)
        sh1, sc1, gt1, sh2, sc2, gt2 = ada_params(c, w_mod[i], b_mod[i])
        a_lat = modulate(h_lat, g_norm_mix[i], sh1, sc1)
        if ctx_in_needed:
            csh1, csc1, cgt1, csh2, csc2, cgt2 = ada_params(c_ctx[None, :], w_mod[i], b_mod[i])
            a_ctx = modulate(h_ctx, g_norm_mix[i], csh1, csc1)
        if m == 0:
            y_lat = fourier_mix(a_lat, w_fnet_out[j])
            y_ctx = None if last else fourier_mix(a_ctx, w_fnet_out[j])
        elif m == 1:
            y_lat, y_ctx = neighbourhood_mix(a_lat, a_ctx, w_na_qkv[j], na_rel_bias[j], w_na_out[j], not last)
        elif m == 2:
            y_lat = spatial_gating_mix(a_lat, w_sg_in[j], g_sg_v[j], w_sg_spatial[j], b_sg_spatial[j], w_sg_out[j])
            y_ctx = None if last else spatial_gating_mix(a_ctx, w_sg_in[j], g_sg_v[j], w_sg_spatial[j],
                                                         b_sg_spatial[j], w_sg_out[j])
        else:
            y_lat, y_ctx = gqa_mix(a_lat, a_ctx, w_att_qkv[j], g_att_q[j], g_att_k[j], w_att_out[j], not last)
        h_lat = h_lat + gt1 * y_lat
        h_lat = h_lat + gt2 * conv_ffn(modulate(h_lat, g_norm_ffn[i], sh2, sc2),
                                       w_ffn_up[i], w_ffn_conv[i], b_ffn_conv[i], w_ffn_down[i])
        if not last:
            h_ctx = h_ctx + cgt1 * y_ctx
            h_ctx = h_ctx + cgt2 * conv_ffn(modulate(h_ctx, g_norm_ffn[i], csh2, csc2),
                                           w_ffn_up[i], w_ffn_conv[i], b_ffn_conv[i], w_ffn_down[i])
    return rms_norm(h_lat, g_final)
```

```python
import numpy as np
import ml_dtypes
import concourse.bass as bass
import concourse.mybir as mybir
from concourse.bass_utils import run_bass_kernel_spmd
from contextlib import ExitStack

F32 = mybir.dt.float32
BF16 = mybir.dt.bfloat16
AF = mybir.ActivationFunctionType
ALU = mybir.AluOpType
AX = mybir.AxisListType

ENGINES = ("pe", "act", "dve", "pool", "sp")
EPOCH = 20000
N_EPOCH_SEMS = 6
DMA_POOL = 12


import types


def freeze(fn):
    if fn.__closure__ is None:
        return fn
    cells = []
    for c in fn.__closure__:
        try:
            cells.append(types.CellType(c.cell_contents))
        except ValueError:
            cells.append(c)
    return types.FunctionType(fn.__code__, fn.__globals__, fn.__name__, fn.__defaults__, tuple(cells))


class Buf:
    __slots__ = ("name", "w", "r")

    def __init__(self, name):
        self.name = name
        self.w = None
        self.r = []


class Op:
    __slots__ = ("eng", "fn", "is_dma", "deps", "inc", "cidx", "dslot", "dval", "gid")


class Sems:
    def __init__(self, nc, st):
        self.csem = {e: [st.enter_context(nc.semaphore(f"c_{e}_{k}")) for k in range(N_EPOCH_SEMS)]
                     for e in ENGINES if e != "sp"}
        self.dsem = {e: [st.enter_context(nc.semaphore(f"d_{e}_{k}")) for k in range(DMA_POOL)]
                     for e in ("sp", "act", "pool")}
        self.cbase = {e: 0 for e in ENGINES}
        self.ndma = {e: 0 for e in ENGINES}


class Prog:
    def __init__(self, nc, sems):
        self.nc = nc
        self.sems = sems
        self.ops = []
        self.streams = {e: [] for e in ENGINES}
        self.ndma = sems.ndma
        self.barrier_deps = {}
        self.dma_ops = []

    def _add(self, eng, fn, reads, writes, is_dma):
        op = Op()
        op.eng, op.fn, op.is_dma = eng, fn, is_dma
        op.inc = False
        op.cidx = None
        op.gid = len(self.ops)
        deps = set()
        for b in reads:
            if b.w is not None:
                deps.add(b.w)
        for b in writes:
            if b.w is not None:
                deps.add(b.w)
            for r in b.r:
                deps.add(r)
        deps.discard(op)
        deps.update(self.barrier_deps.pop(eng, ()))
        keep = {}
        op.deps = []
        for d in deps:
            if d.is_dma:
                op.deps.append(d)
            elif d.eng == "pe" and eng == "pe" and not is_dma:
                continue
            elif d.eng not in keep or keep[d.eng].gid < d.gid:
                keep[d.eng] = d
        op.deps.extend(keep.values())
        for d in op.deps:
            d.inc = True
        for b in reads:
            b.r.append(op)
        for b in writes:
            b.w = op
            b.r = []
        if is_dma:
            j = self.ndma[eng]
            self.ndma[eng] += 1
            op.dslot = j % DMA_POOL
            op.dval = 16 * (j // DMA_POOL + 1)
            self.dma_ops.append(op)
        self.ops.append(op)
        self.streams[eng].append(op)
        return op

    def op(self, eng, fn, reads=(), writes=()):
        return self._add(eng, freeze(fn), list(reads), list(writes), False)

    def dma(self, eng, out, in_, reads=(), writes=()):
        def fn(e):
            return e.dma_start(out=out, in_=in_)
        return self._add(eng, fn, list(reads), list(writes), True)

    def emit(self, final_wait_ops=()):
        nc = self.nc
        with ExitStack() as st:
            csem, dsem = self.sems.csem, self.sems.dsem
            for e in ENGINES:
                k = self.sems.cbase[e]
                for op in self.streams[e]:
                    if not op.is_dma and op.inc:
                        k += 1
                        op.cidx = k
                self.sems.cbase[e] = k
                assert k < EPOCH * N_EPOCH_SEMS, (e, k)
            block = st.enter_context(nc.Block(no_gpsimd_drain=True))

            def run_stream(ename, e):
                seen_c = {}
                seen_d = {}
                last_on_slot = {}

                def wait_for(d):
                    if d.is_dma:
                        key = (d.eng, d.dslot)
                        if seen_d.get(key, 0) >= d.dval:
                            return
                        seen_d[key] = d.dval
                        e.wait_ge(dsem[d.eng][d.dslot], d.dval)
                    else:
                        if seen_c.get(d.eng, 0) >= d.cidx:
                            return
                        seen_c[d.eng] = d.cidx
                        ep, v = divmod(d.cidx - 1, EPOCH)
                        e.wait_ge(csem[d.eng][ep], v + 1)

                for op in self.streams[ename]:
                    for d in sorted(op.deps, key=lambda o: o.gid):
                        wait_for(d)
                    if op.is_dma:
                        prev = last_on_slot.get(op.dslot)
                        if prev is not None:
                            wait_for(prev)
                        last_on_slot[op.dslot] = op
                        ins = op.fn(e)
                        ins.then_inc(dsem[ename][op.dslot], 16)
                    else:
                        ins = op.fn(e)
                        if op.inc:
                            ep, _ = divmod(op.cidx - 1, EPOCH)
                            ins.then_inc(csem[ename][ep], 1)
                if ename == "sp":
                    for d in final_wait_ops:
                        wait_for(d)

            block.sync(lambda e: run_stream("sp", e))
            block.scalar(lambda e: run_stream("act", e))
            block.vector(lambda e: run_stream("dve", e))
            block.gpsimd(lambda e: run_stream("pool", e))
            block.tensor(lambda e: run_stream("pe", e))


class Stage:
    def __init__(self, nc, name):
        self.nc, self.name = nc, name

    def __enter__(self):
        self.st = ExitStack()
        if SCOPES[0]:
            self.st.enter_context(self.nc.named_scope(self.name))
        self.P = Prog(self.nc, SEMS[0])
        self.npsum = 0
        self.psums = []
        return self

    def sb(self, name, shape, dt):
        t = self.st.enter_context(self.nc.sbuf_tensor(f"{self.name}_{name}", list(shape), dt))
        return t, Buf(name)

    def sbs(self, name, shape, dt, n):
        return [self.sb(f"{name}{i}", shape, dt) for i in range(n)]

    def psum_pool(self, n, shape=(128, 512), dt=F32):
        self.psums = [(self.st.enter_context(self.nc.psum_tensor(f"{self.name}_ps{i}", list(shape), dt)),
                       Buf(f"ps{i}")) for i in range(n)]
        self.pi = 0

    def ps(self):
        r = self.psums[self.pi % len(self.psums)]
        self.pi += 1
        return r

    def __exit__(self, et, ev, tb):
        if et is None:
            P = self.P
            tail = []
            for q in ("sp", "act", "pool"):
                tail.extend([d for d in P.dma_ops if d.eng == q][-DMA_POOL:])
            P.emit(final_wait_ops=tail)
        self.st.close()
        return False


class RR:
    def __init__(self, items):
        self.items, self.i = items, 0

    def next(self):
        r = self.items[self.i % len(self.items)]
        self.i += 1
        return r


D = 1024
KC = 8
TL = 4096
TCX = 256
T = TL + TCX
FF = 2816
FC = 22
DEPTH = 4
EPS = 1e-6


def col_tiles(lo, hi, w=512, slack=0):
    out = []
    while lo < hi:
        n = min(w, hi - lo)
        if hi - lo - n <= slack:
            n = hi - lo
        out.append((lo, n))
        lo += n
    return out


def hview(h):
    return h.rearrange("(kc p) t -> p kc t", p=128)


def emit_norm_a(S, hsb, hb, n, ssq):
    sq_t, sq_b = ssq
    for kc in range(KC):
        S.P.op("act", lambda e, kc=kc: e.activation(sq_t[:, kc, 0:n], hsb[:, kc, 0:n], AF.Square), reads=[hb], writes=[sq_b])


def emit_norm_b(S, hsb, hb, n, asb, ab, aoff, A, Bv, ones, ssq, rstd, tmp, tb, mi, tabb=(), ones_b=None):
    P = S.P
    sq_t, sq_b = ssq
    r_t, r_b = rstd
    ps_t, ps_b = S.ps()
    for kc in range(KC):
        P.op("pe", lambda e, kc=kc: e.matmul(ps_t[:, 0:n], ones[:], sq_t[:, kc, 0:n], start=(kc == 0), stop=(kc == KC - 1)),
             reads=[sq_b, ones_b], writes=[ps_b])
    P.op("act", lambda e: e.activation(r_t[:, 0:n], ps_t[:, 0:n], AF.Sqrt, bias=EPS, scale=1.0 / D), reads=[ps_b], writes=[r_b])
    P.op("dve", lambda e: e.reciprocal(r_t[:, 0:n], r_t[:, 0:n]), reads=[r_b], writes=[r_b])
    for kc in range(KC):
        P.op("dve", lambda e, kc=kc: e.scalar_tensor_tensor(tmp[:, kc, 0:n], hsb[:, kc, 0:n], A[:, kc, mi:mi + 1], r_t[:, 0:n],
                                                           ALU.mult, ALU.mult), reads=[hb, r_b, *tabb], writes=[tb[kc]])
        if Bv is None:
            continue
        P.op("act", lambda e, kc=kc: e.activation(asb[:, kc, aoff:aoff + n], tmp[:, kc, 0:n], AF.Identity,
                                                  bias=Bv[:, kc, mi:mi + 1], scale=1.0), reads=[tb[kc]], writes=[ab])


class NormCtx:
    def __init__(self, S, w=512, nh=2, nb=2):
        self.S = S
        w = w + 4
        self.w = w
        self.h = RR(S.sbs("nh", [128, KC, w], F32, nh))
        self.sq = RR(S.sbs("nsq", [128, KC, w], BF16, nb))
        self.rs = RR(S.sbs("nrs", [128, w], F32, nb))
        self.tmp = RR([(t, [Buf(f"tmp{i}_{k}") for k in range(KC)]) for i, (t, _) in
                       enumerate(S.sbs("ntmp", [128, KC, w], F32, nb))])
        self.ones, self.ones_b = S.sb("ones", [128, 128], BF16)
        S.P.op("pool", lambda e: e.memset(self.ones[:], 1.0), writes=[self.ones_b])

    def run_a(self, hsrc, t0, n):
        S = self.S
        hv, hbuf = hsrc
        (h_t, h_b) = self.h.next()
        S.P.dma("sp", h_t[:, :, 0:n], hv[:, :, t0:t0 + n], reads=[hbuf], writes=[h_b])
        ssq = self.sq.next()
        emit_norm_a(S, h_t, h_b, n, ssq)
        return (h_t, h_b, ssq, n)

    def run_b(self, st, asb, ab, aoff, A, Bv, mi, tabb=()):
        h_t, h_b, ssq, n = st
        tmp_t, tmp_b = self.tmp.next()
        emit_norm_b(self.S, h_t, h_b, n, asb, ab, aoff, A, Bv, self.ones, ssq, self.rs.next(), tmp_t, tmp_b, mi, tabb, self.ones_b)
        return h_t, h_b, tmp_t, tmp_b

    def run(self, hsrc, t0, n, asb, ab, aoff, A, Bv, mi, tabb=()):
        return self.run_b(self.run_a(hsrc, t0, n), asb, ab, aoff, A, Bv, mi, tabb)


def stage_ada(nc, S, condT, w_mod, bmod2, gmixT, gffnT, ident2, tabs):
    P = S.P
    S.psum_pool(8)
    rowp = RR(S.psums[0:4])
    trp = RR(S.psums[4:8])
    c_t, c_b = S.sb("c", [128, KC, 2], F32)
    sc_t, sc_b = S.sb("sc", [128, KC, 2], F32)
    bm_t, bm_b = S.sb("bm", [2, DEPTH, 6 * D], F32)
    gm_t, gm_b = S.sb("gm", [128, DEPTH, KC], F32)
    gf_t, gf_b = S.sb("gf", [128, DEPTH, KC], F32)
    mod_t, mod_b = S.sb("mod", [128, DEPTH, 48, 2], F32)
    id_t, id_b = S.sb("id2", [2, 2], F32)
    P.dma("sp", c_t[:], condT, writes=[c_b])
    P.dma("sp", bm_t[:], bmod2, writes=[bm_b])
    P.dma("sp", gm_t[:], gmixT, writes=[gm_b])
    P.dma("sp", gf_t[:], gffnT, writes=[gf_b])
    P.dma("sp", id_t[:], ident2, writes=[id_b])
    P.op("act", lambda e: e.activation(sc_t[:], c_t[:], AF.Silu), reads=[c_b], writes=[sc_b])
    NW = 1024
    wbufs = RR(S.sbs("w", [128, KC, NW], F32, 3))
    rows = RR(S.sbs("row", [2, NW], F32, 3))
    qi = 0
    for i in range(DEPTH):
        wv = w_mod[i].rearrange("(kc p) f -> p kc f", p=128)
        for g in range(6 * D // NW):
            w_t, w_b = wbufs.next()
            P.dma(("sp", "act")[qi % 2], w_t[:], wv[:, :, g * NW:(g + 1) * NW], writes=[w_b])
            qi += 1
            r_t, r_b = rows.next()
            for hh in range(NW // 512):
                ps_t, ps_b = rowp.next()
                for kc in range(KC):
                    P.op("pe", lambda e, kc=kc: e.matmul(ps_t[0:2, :], sc_t[:, kc, :], w_t[:, kc, hh * 512:(hh + 1) * 512],
                                                        start=(kc == 0), stop=(kc == KC - 1)), reads=[w_b, sc_b], writes=[ps_b])
                f0 = g * NW + hh * 512
                P.op("dve", lambda e: e.tensor_tensor(r_t[0:2, hh * 512:(hh + 1) * 512], ps_t[0:2, :], bm_t[0:2, i, f0:f0 + 512], ALU.add),
                     reads=[ps_b, bm_b], writes=[r_b])
            tp_t, tp_b = trp.next()
            for jj in range(NW // 128):
                P.op("pe", lambda e, jj=jj: e.matmul(tp_t[:, 2 * jj:2 * jj + 2], r_t[0:2, jj * 128:(jj + 1) * 128], id_t[0:2, 0:2],
                                                    start=True, stop=True), reads=[r_b, id_b], writes=[tp_b])
            j0 = g * (NW // 128)
            P.op("dve", lambda e: e.tensor_copy(mod_t[:, i, j0:j0 + NW // 128, :],
                                                tp_t[:, 0:2 * (NW // 128)].rearrange("p (j r) -> p j r", r=2)), reads=[tp_b], writes=[mod_b])
    tb = tabs["buf"]
    for i in range(DEPTH):
        for (an, bn, gn, g_t, base) in (("A1", "B1", "G1", gm_t, 0), ("A2", "B2", "G2", gf_t, 24)):
            for kc in range(KC):
                P.op("dve", lambda e, i=i, kc=kc, an=an, g_t=g_t, base=base: e.tensor_scalar(
                    tabs[an][:, i, kc, :], mod_t[:, i, base + 8 + kc, :], 1.0, g_t[:, i, kc:kc + 1], ALU.add, ALU.mult),
                    reads=[mod_b], writes=[tb])
            P.op("dve", lambda e, i=i, bn=bn, base=base: e.tensor_copy(tabs[bn][:, i, :, :], mod_t[:, i, base:base + 8, :]),
                 reads=[mod_b], writes=[tb])
            P.op("dve", lambda e, i=i, gn=gn, base=base: e.tensor_copy(tabs[gn][:, i, :, :], mod_t[:, i, base + 16:base + 24, :]),
                 reads=[mod_b], writes=[tb])


def ffn_blocks(last):
    blocks = [[(0, 1024, 0, 2, 0)], [(1024, 2048, 2, 2, 0)], [(2048, 3072, 2, 2, 0)]]
    if last:
        blocks.append([(3072, 4096, 2, 0, 0)])
    else:
        blocks.append([(3072, 4096, 2, 0, 0), (4096, 4352, 0, 0, 1)])
    return blocks


def stage_ffn(nc, S, li, last, hmid, hout, w_up, w_down, convT, tabs):
    P = S.P
    S.psum_pool(8)
    NB = 1024 + 4
    nrm = NormCtx(S, w=256)
    a_t, a_b = S.sb("a2", [128, KC, NB + 256], BF16)
    u_t, u_b = S.sb("u", [128, FC, 1280], BF16)
    wd_t, wd_b = S.sb("wd", [128, FC, D], BF16)
    cv_t, cv_b = S.sb("cv", [128, 4, FC], F32)
    gfull = RR(S.sbs("g", [128, NB], F32, 2))
    cbuf = RR(S.sbs("c", [128, 512], F32, 2))
    sbuf_ = RR(S.sbs("s", [128, 512], BF16, 2))
    wu = RR(S.sbs("wu", [128, KC, 256], BF16, 3))
    obuf = RR(S.sbs("o", [128, 512], F32, 3))
    hres = RR(S.sbs("hr", [128, 512], F32, 3))
    P.dma("sp", cv_t[:], convT[li], writes=[cv_b])
    P.dma("pool", wd_t[:], w_down[li].rearrange("(fc p) d -> p fc d", p=128), writes=[wd_b])
    wuv = w_up[li].rearrange("(kc p) f -> p kc f", p=128)
    A, Bv, G = tabs["A2"], tabs["B2"], tabs["G2"]
    tb = tabs["buf"]
    hv, hbuf = hmid
    ov, obuf_d = hout
    blocks = ffn_blocks(last)

    def plan(blk):
        segs, ntiles, off = [], [], 0
        for (s_, e_, hl, hr, mi) in blk:
            lo, hi = s_ - hl, e_ + hr
            for (t0, n) in col_tiles(lo, hi, 256, 4):
                ntiles.append((t0, n, off + (t0 - lo), mi))
            segs.append((s_, e_, hl, hr, mi, off, lo, hi))
            off += hi - lo
        return segs, ntiles

    def norm_tile(t0, n, aoff, mi):
        nrm.run((hv, hbuf), t0, n, a_t, a_b, aoff, A[:, li], Bv[:, li], mi)

    plans = [plan(blk) for blk in blocks]
    for nt in plans[0][1]:
        norm_tile(*nt)
    gctx = RR(S.sbs("gc", [128, TCX + 4], F32, 2))
    for (g_t, g_b) in gctx.items:
        P.op("pool", lambda e: e.memset(g_t[:, 0:2], 0.0), writes=[g_b])
        P.op("pool", lambda e: e.memset(g_t[:, TCX + 2:TCX + 4], 0.0), writes=[g_b])
    for bi, (segs, _) in enumerate(plans):
        for (s, e_, hl, hr, mi, off, lo, hi) in segs:
            if mi == 1:
                continue
            for (g_t, g_b) in gfull.items:
                if not hl:
                    P.op("pool", lambda e: e.memset(g_t[:, 0:2], 0.0), writes=[g_b])
                if not hr:
                    P.op("pool", lambda e: e.memset(g_t[:, e_ - s + 2:e_ - s + 4], 0.0), writes=[g_b])
        for j in range(FC):
            w_t, w_b = wu.next()
            P.dma("pool", w_t[:, :, 0:128], wuv[:, :, j * 128:(j + 1) * 128], writes=[w_b])
            P.dma("pool", w_t[:, :, 128:256], wuv[:, :, FF + j * 128:FF + (j + 1) * 128], writes=[w_b])
            uoff = 0
            for (s, e_, hl, hr, mi, off, lo, hi) in segs:
                g_t, g_b = (gctx if mi == 1 else gfull).next()
                nseg = e_ - s
                for (t0, n) in col_tiles(lo, hi):
                    ps_t, ps_b = S.ps()
                    for kc in range(KC):
                        P.op("pe", lambda e, kc=kc, w_t=w_t, ps_t=ps_t, c0=off + t0 - lo, n=n: e.matmul(
                            ps_t[:, 0:n], w_t[:, kc, 0:128], a_t[:, kc, c0:c0 + n], start=(kc == 0), stop=(kc == KC - 1)),
                            reads=[w_b, a_b], writes=[ps_b])
                    gc = t0 - s + 2
                    P.op("act", lambda e, g_t=g_t, ps_t=ps_t, gc=gc, n=n: e.copy(g_t[:, gc:gc + n], ps_t[:, 0:n]),
                         reads=[ps_b], writes=[g_b])
                for (t0, n) in col_tiles(s, e_):
                    ps_t, ps_b = S.ps()
                    for kc in range(KC):
                        P.op("pe", lambda e, kc=kc, w_t=w_t, ps_t=ps_t, c0=off + t0 - lo, n=n: e.matmul(
                            ps_t[:, 0:n], w_t[:, kc, 128:256], a_t[:, kc, c0:c0 + n], start=(kc == 0), stop=(kc == KC - 1)),
                            reads=[w_b, a_b], writes=[ps_b])
                    gc = t0 - s + 2
                    c_t, c_b = cbuf.next()
                    s_t, s_b = sbuf_.next()
                    P.op("act", lambda e, c_t=c_t, g_t=g_t, gc=gc, n=n, j=j: e.activation(
                        c_t[:, 0:n], g_t[:, gc:gc + n], AF.Identity, bias=cv_t[:, 3, j:j + 1], scale=cv_t[:, 1, j:j + 1]),
                        reads=[g_b, cv_b], writes=[c_b])
                    P.op("dve", lambda e, c_t=c_t, g_t=g_t, gc=gc, n=n, j=j: e.scalar_tensor_tensor(
                        c_t[:, 0:n], g_t[:, gc - 1:gc - 1 + n], cv_t[:, 0, j:j + 1], c_t[:, 0:n], ALU.mult, ALU.add),
                        reads=[g_b, cv_b, c_b], writes=[c_b])
                    P.op("dve", lambda e, c_t=c_t, g_t=g_t, gc=gc, n=n, j=j: e.scalar_tensor_tensor(
                        c_t[:, 0:n], g_t[:, gc + 1:gc + 1 + n], cv_t[:, 2, j:j + 1], c_t[:, 0:n], ALU.mult, ALU.add),
                        reads=[g_b, cv_b, c_b], writes=[c_b])
                    P.op("act", lambda e, c_t=c_t, s_t=s_t, n=n: e.activation(s_t[:, 0:n], c_t[:, 0:n], AF.Silu),
                         reads=[c_b], writes=[s_b])
                    uc = uoff + t0 - s
                    P.op("dve", lambda e, s_t=s_t, ps_t=ps_t, uc=uc, n=n, j=j: e.tensor_tensor(
                        u_t[:, j, uc:uc + n], s_t[:, 0:n], ps_t[:, 0:n], ALU.mult), reads=[s_b, ps_b], writes=[u_b])
                uoff += nseg
        pending = list(plans[bi + 1][1]) if bi + 1 < len(plans) else []
        groups = []
        uoff = 0
        for (s, e_, hl, hr, mi, off, lo, hi) in segs:
            for (t0, n) in col_tiles(s, e_):
                for dc in range(KC):
                    groups.append((t0, n, dc, mi, uoff + t0 - s))
            uoff += e_ - s
        every = max(1, len(groups) // (len(pending) + 2)) if pending else 0
        started = []
        for gi, (t0, n, dc, mi, uc) in enumerate(groups):
            h_t, h_b = hres.next()
            P.dma("sp", h_t[:, 0:n], hv[:, dc, t0:t0 + n], reads=[hbuf], writes=[h_b])
            o_t, o_b = obuf.next()
            ps_t, ps_b = S.ps()
            for fc in range(FC):
                P.op("pe", lambda e, fc=fc: e.matmul(
                    ps_t[:, 0:n], wd_t[:, fc, dc * 128:(dc + 1) * 128], u_t[:, fc, uc:uc + n],
                    start=(fc == 0), stop=(fc == FC - 1)), reads=[wd_b, u_b], writes=[ps_b])
            P.op("dve", lambda e: e.scalar_tensor_tensor(
                o_t[:, 0:n], ps_t[:, 0:n], G[:, li, dc, mi:mi + 1], h_t[:, 0:n], ALU.mult, ALU.add),
                reads=[ps_b, h_b], writes=[o_b])
            P.dma("act", ov[:, dc, t0:t0 + n], o_t[:, 0:n], reads=[o_b], writes=[Buf("od")])
            if every and (gi + 1) % every == 0:
                if started:
                    (nt, st_) = started.pop(0)
                    nrm.run_b(st_, a_t, a_b, nt[2], A[:, li], Bv[:, li], nt[3])
                if pending:
                    nt = pending.pop(0)
                    started.append((nt, nrm.run_a((hv, hbuf), nt[0], nt[1])))
        for (nt, st_) in started:
            nrm.run_b(st_, a_t, a_b, nt[2], A[:, li], Bv[:, li], nt[3])
        for nt in pending:
            norm_tile(*nt)


def stage_final(nc, S, hin, outT, gfinT):
    P = S.P
    S.psum_pool(4)
    nrm = NormCtx(S, w=512)
    gf_t, gf_b = S.sb("gfin", [128, KC, 1], F32)
    P.dma("sp", gf_t[:], gfinT, writes=[gf_b])
    ov = outT.rearrange("(kc p) t -> p kc t", p=128)
    ob = Buf("out")
    for (t0, n) in col_tiles(0, TL, 512):
        h_t, h_b, tmp_t, tmp_b = nrm.run(hin, t0, n, None, None, 0, gf_t, None, 0, tabb=[gf_b])
        P.dma("act", ov[:, :, t0:t0 + n], tmp_t[:, :, 0:n], reads=tmp_b, writes=[Buf("od")])


def stage_copy_mixer(nc, S, hin, hmid, last):
    hv, hb = hin
    mv, mb = hmid
    for (t0, n) in col_tiles(0, TL if last else T, 1088):
        S.P.dma("sp", mv[:, :, t0:t0 + n], hv[:, :, t0:t0 + n], writes=[Buf("x")])


MIXERS = {}
EXTRA_INPUTS = [
    ("w_sg_in", [D, 2 * D], F32), ("gsgT", [128, KC], F32), ("wsT", [128, 4, 128], F32), ("bsR", [128, 4, 128], F32),
    ("w_sg_out", [D, D], F32),
]
EXTRA_INPUTS += [
    ("w_fnet_out", [D, D], F32), ("cs256", [256, 512], BF16), ("csn256", [256, 512], BF16),
    ("dftC", [TL, TL], BF16), ("dftSn", [TL, TL], BF16),
]
EXTRA_INPUTS += [
    ("w_na_qkv", [D, 3 * D], F32), ("w_na_out", [D, D], F32), ("nab", [16, 64, 15 * 64], F32),
    ("w_att_qkv", [D, 1536], F32), ("w_att_out", [D, D], F32), ("gqk", [128, 2], F32),
    ("ident64", [64, 64], BF16), ("rotT", [128, 128], BF16), ("bd64", [128, 128], BF16), ("cosF", [128, T], F32), ("sinF", [128, T], F32),
]
SCRATCH = [("Zd", [T, 4, 512], BF16), ("YT", [D, T], BF16),
           ("QH", [16, 64, T], BF16), ("KH", [16, 64, T], BF16), ("VT", [T, D], BF16), ("OH", [16, 64, T], BF16)]


def emit_outproj_residual(S, li, src_t, src_b, w_t, w_b, h_t, h_b, n, t0, mi, G, ov, obufs):
    P = S.P
    for dc in range(KC):
        ps_t, ps_b = S.ps()
        for c in range(KC):
            P.op("pe", lambda e, c=c, dc=dc, ps_t=ps_t: e.matmul(ps_t[:, 0:n], w_t[:, c, dc * 128:(dc + 1) * 128], src_t[:, c, 0:n],
                                                               start=(c == 0), stop=(c == KC - 1)), reads=[w_b, src_b], writes=[ps_b])
        o_t, o_b = obufs.next()
        P.op("dve", lambda e, dc=dc, ps_t=ps_t, o_t=o_t: e.scalar_tensor_tensor(
            o_t[:, 0:n], ps_t[:, 0:n], G[:, li, dc, mi:mi + 1], h_t[:, dc, 0:n], ALU.mult, ALU.add), reads=[ps_b, h_b], writes=[o_b])
        P.dma("act", ov[:, dc, t0:t0 + n], o_t[:, 0:n], reads=[o_b], writes=[Buf("od")])


def mixer_sg(nc, li, last, hin, hmid, I, scratch, tabs):
    with Stage(nc, f"sg{li}") as S:
        P = S.P
        S.psum_pool(8)
        nrm = NormCtx(S, w=512, nh=2, nb=1)
        wu_t, wu_b = S.sb("wu", [128, KC, D], BF16)
        wv_t, wv_b = S.sb("wv", [128, KC, D], BF16)
        wo_t, wo_b = S.sb("wo", [128, KC, D], BF16)
        ws_t, ws_b = S.sb("ws", [128, 4, 128], BF16)
        bs_t, bs_b = S.sb("bs", [128, 4, 128], F32)
        gv_t, gv_b = S.sb("gv", [128, KC], F32)
        wiv = I["w_sg_in"].rearrange("(kc p) f -> p kc f", p=128)
        P.dma("pool", wu_t[:], wiv[:, :, 0:D], writes=[wu_b])
        P.dma("pool", wv_t[:], wiv[:, :, D:2 * D], writes=[wv_b])
        P.dma("pool", wo_t[:], I["w_sg_out"].rearrange("(kc p) f -> p kc f", p=128), writes=[wo_b])
        P.dma("pool", ws_t[:], I["wsT"], writes=[ws_b])
        P.dma("sp", bs_t[:], I["bsR"], writes=[bs_b])
        P.dma("sp", gv_t[:], I["gsgT"], writes=[gv_b])
        abufs = RR(S.sbs("a", [128, KC, 512], BF16, 2))
        u_t, u_b = S.sb("u", [128, KC, 512], F32)
        mx_t, mx_b = S.sb("mx", [128, KC, 512], F32)
        gt_t, gt_b = S.sb("gt", [128, KC, 512], BF16)
        vgs = RR(S.sbs("vg", [128, D], F32, 4))
        vss = RR(S.sbs("vs", [128, D], BF16, 4))
        sss = RR(S.sbs("ss", [128, 4], F32, 4))
        junk_t, junk_b = S.sb("junk", [128, D], BF16)
        obufs = RR(S.sbs("o", [128, 512], F32, 3))
        A, Bv, G = tabs["A1"], tabs["B1"], tabs["G1"]
        hv, hbuf = hin
        ov, _ = hmid
        tiles = [(t0, n, 0) for (t0, n) in col_tiles(0, TL, 512)] + ([] if last else [(TL, TCX, 1)])
        for (t0, n, mi) in tiles:
            a_t, a_b = abufs.next()
            h_t, h_b, _, _ = nrm.run((hv, hbuf), t0, n, a_t, a_b, 0, A[:, li], Bv[:, li], mi)
            for c in range(KC):
                ps_t, ps_b = S.ps()
                for kc in range(KC):
                    P.op("pe", lambda e, c=c, kc=kc, ps_t=ps_t, a_t=a_t: e.matmul(
                        ps_t[:, 0:n], wu_t[:, kc, c * 128:(c + 1) * 128], a_t[:, kc, 0:n], start=(kc == 0), stop=(kc == KC - 1)),
                        reads=[wu_b, a_b], writes=[ps_b])
                P.op("act", lambda e, c=c, ps_t=ps_t: e.activation(u_t[:, c, 0:n], ps_t[:, 0:n], AF.Gelu_apprx_tanh),
                     reads=[ps_b], writes=[u_b])
            def sg_a(q):
                vg_t, vg_b = vgs.next()
                vs_t, vs_b = vss.next()
                ss_t, ss_b = sss.next()
                P.op("pool", lambda e: e.memset(ss_t[:, 0:2], 0.0), writes=[ss_b])
                for half in range(2):
                    ps_t, ps_b = S.ps()
                    for kc in range(KC):
                        P.op("pe", lambda e, kc=kc: e.matmul(
                            ps_t[:, :], a_t[:, kc, q * 128:(q + 1) * 128], wv_t[:, kc, half * 512:(half + 1) * 512],
                            start=(kc == 0), stop=(kc == KC - 1)), reads=[wv_b, a_b], writes=[ps_b])
                    P.op("act", lambda e: e.activation(
                        vg_t[:, half * 512:(half + 1) * 512], ps_t[:, :], AF.Gelu_apprx_tanh), reads=[ps_b], writes=[vg_b])
                P.op("act", lambda e: e.activation(
                    junk_t[:, :], vg_t[:, :], AF.Square, accum_out=ss_t[:, 0:1]), reads=[vg_b, ss_b], writes=[ss_b, junk_b])
                P.op("act", lambda e: e.activation(ss_t[:, 2:4], ss_t[:, 0:2], AF.Sqrt, bias=EPS, scale=1.0 / D), reads=[ss_b], writes=[ss_b])
                P.op("dve", lambda e: e.reciprocal(ss_t[:, 2:4], ss_t[:, 2:4]), reads=[ss_b], writes=[ss_b])
                P.op("dve", lambda e: e.tensor_scalar(vs_t[:, :], vg_t[:, :], ss_t[:, 2:3], None, ALU.mult),
                     reads=[ss_b, vg_b], writes=[vs_b])
                return vs_t, vs_b

            def sg_b(q, vs_t, vs_b):
                for c4 in range(2):
                    ps_t, ps_b = S.ps()
                    for cc in range(4):
                        c = c4 * 4 + cc
                        P.op("pe", lambda e: e.matmul(
                            ps_t[:, cc * 128:(cc + 1) * 128], vs_t[:, c * 128:(c + 1) * 128], ws_t[:, c // 2, :], start=True, stop=True),
                            reads=[vs_b, ws_b], writes=[ps_b])
                    for cc in range(4):
                        c = c4 * 4 + cc
                        P.op("dve", lambda e: e.scalar_tensor_tensor(
                            mx_t[:, c, q * 128:(q + 1) * 128], ps_t[:, cc * 128:(cc + 1) * 128], gv_t[:, c:c + 1], bs_t[:, c // 2, :],
                            ALU.mult, ALU.add), reads=[ps_b, gv_b, bs_b], writes=[mx_b])

            nq_ = n // 128
            prev = sg_a(0)
            for q in range(1, nq_):
                cur = sg_a(q)
                sg_b(q - 1, *prev)
                prev = cur
            sg_b(nq_ - 1, *prev)
            for c in range(KC):
                P.op("dve", lambda e, c=c: e.tensor_tensor(gt_t[:, c, 0:n], u_t[:, c, 0:n], mx_t[:, c, 0:n], ALU.mult),
                     reads=[u_b, mx_b], writes=[gt_b])
            emit_outproj_residual(S, li, gt_t, gt_b, wo_t, wo_b, h_t, h_b, n, t0, mi, G, ov, obufs)


MIXERS[2] = mixer_sg


def mixer_fnet(nc, li, last, hin, hmid, I, scratch, tabs):
    Zd, YT = scratch["Zd"], scratch["YT"]
    A, Bv, G = tabs["A1"], tabs["B1"], tabs["G1"]
    hv, hbuf = hin
    ov, _ = hmid
    tiles = [(t0, n, 0) for (t0, n) in col_tiles(0, TL, 256)] + ([] if last else [(TL, TCX, 1)])
    with Stage(nc, f"fn1_{li}") as S:
        P = S.P
        S.psum_pool(8)
        nrm = NormCtx(S, w=256, nh=2, nb=2)
        cs_t, cs_b = S.sb("cs", [128, 2, 512], BF16)
        P.dma("sp", cs_t[:], I["cs256"].rearrange("(k p) c -> p k c", p=128), writes=[cs_b])
        abufs = RR(S.sbs("a", [128, KC, 256], BF16, 2))
        zbufs = RR(S.sbs("z", [128, 4, 512], BF16, 3))
        for (t0, n, mi) in tiles:
            a_t, a_b = abufs.next()
            nrm.run((hv, hbuf), t0, n, a_t, a_b, 0, A[:, li], Bv[:, li], mi)
            for q in range(n // 128):
                z_t, z_b = zbufs.next()
                for g in range(4):
                    ps_t, ps_b = S.ps()
                    for kk in range(2):
                        P.op("pe", lambda e, g=g, kk=kk, q=q, ps_t=ps_t, a_t=a_t: e.matmul(
                            ps_t[:, :], a_t[:, 2 * g + kk, q * 128:(q + 1) * 128], cs_t[:, kk, :], start=(kk == 0), stop=(kk == 1)),
                            reads=[a_b, cs_b], writes=[ps_b])
                    P.op("act" if g % 2 == 0 else "dve",
                         (lambda e, g=g, ps_t=ps_t, z_t=z_t: e.copy(z_t[:, g, :], ps_t[:, :])) if g % 2 == 0 else
                         (lambda e, g=g, ps_t=ps_t, z_t=z_t: e.tensor_copy(z_t[:, g, :], ps_t[:, :])),
                         reads=[ps_b], writes=[z_b])
                r0 = t0 + q * 128
                P.dma("sp", Zd[r0:r0 + 128, :, :], z_t[:], reads=[z_b], writes=[Buf("zd")])
    with Stage(nc, f"fn2_{li}") as S:
        P = S.P
        S.psum_pool(8)
        z_t, z_b = S.sb("z", [128, 32, 2, 512], BF16)
        cbufs = RR(S.sbs("c", [128, 8, 512], BF16, 3))
        sbufs = RR(S.sbs("s", [128, 8, 512], BF16, 3))
        ybufs = RR(S.sbs("y", [128, 512], BF16, 4))
        seqs = [(0, TL, I["dftC"], I["dftSn"], 512)]
        if not last:
            seqs.append((TL, TCX, I["csn256"][:, 0:256], I["csn256"][:, 256:512], 256))
        for (tok0, L, Cm, Sm, kbw) in seqs:
            nch = L // 128
            Cv = Cm.rearrange("(nc p) k -> p nc k", p=128)
            Sv = Sm.rearrange("(nc p) k -> p nc k", p=128)
            Zv = Zd[tok0:tok0 + L, :, :].rearrange("(nc p) g c -> p nc g c", p=128)
            for gp in range(2):
                P.dma("sp", z_t[:, 0:nch, :, :], Zv[:, :, 2 * gp:2 * gp + 2, :], writes=[z_b])
                for kb in range(L // kbw):
                    accs = [S.ps() for _ in range(4)]
                    npc = min(8, nch)
                    for pc in range(nch // npc):
                        c_t, c_b = cbufs.next()
                        s_t, s_b = sbufs.next()
                        P.dma("sp", c_t[:, 0:npc, 0:kbw], Cv[:, pc * npc:(pc + 1) * npc, kb * kbw:(kb + 1) * kbw], writes=[c_b])
                        P.dma("act", s_t[:, 0:npc, 0:kbw], Sv[:, pc * npc:(pc + 1) * npc, kb * kbw:(kb + 1) * kbw], writes=[s_b])
                        for nn in range(npc):
                            ni = pc * npc + nn
                            for ai, (ps_t, ps_b) in enumerate(accs):
                                gg, cc = ai // 2, ai % 2
                                P.op("pe", lambda e, ps_t=ps_t, ni=ni, nn=nn, gg=gg, cc=cc, c_t=c_t: e.matmul(
                                    ps_t[:, 0:kbw], z_t[:, ni, gg, cc * 128:(cc + 1) * 128], c_t[:, nn, 0:kbw],
                                    start=(ni == 0), stop=False), reads=[z_b, c_b], writes=[ps_b])
                                P.op("pe", lambda e, ps_t=ps_t, ni=ni, nn=nn, gg=gg, cc=cc, s_t=s_t: e.matmul(
                                    ps_t[:, 0:kbw], z_t[:, ni, gg, 256 + cc * 128:256 + (cc + 1) * 128], s_t[:, nn, 0:kbw],
                                    start=False, stop=(ni == nch - 1)), reads=[z_b, s_b], writes=[ps_b])
                    for ai, (ps_t, ps_b) in enumerate(accs):
                        gg, cc = ai // 2, ai % 2
                        chunk = (2 * gp + gg) * 2 + cc
                        y_t, y_b = ybufs.next()
                        if ai % 2 == 0:
                            P.op("act", lambda e, ps_t=ps_t, y_t=y_t: e.copy(y_t[:, 0:kbw], ps_t[:, 0:kbw]), reads=[ps_b], writes=[y_b])
                        else:
                            P.op("dve", lambda e, ps_t=ps_t, y_t=y_t: e.tensor_copy(y_t[:, 0:kbw], ps_t[:, 0:kbw]), reads=[ps_b], writes=[y_b])
                        c0 = tok0 + kb * kbw
                        P.dma("pool", YT[chunk * 128:(chunk + 1) * 128, c0:c0 + kbw], y_t[:, 0:kbw], reads=[y_b], writes=[Buf("yt")])
    with Stage(nc, f"fn3_{li}") as S:
        P = S.P
        S.psum_pool(8)
        wo_t, wo_b = S.sb("wo", [128, KC, D], BF16)
        P.dma("pool", wo_t[:], I["w_fnet_out"].rearrange("(kc p) f -> p kc f", p=128), writes=[wo_b])
        hb_ = RR(S.sbs("h", [128, KC, 256], F32, 2))
        yb_ = RR(S.sbs("y", [128, KC, 256], BF16, 2))
        obufs = RR(S.sbs("o", [128, 256], F32, 4))
        YTv = YT.rearrange("(kc p) t -> p kc t", p=128)
        for (t0, n, mi) in tiles:
            h_t, h_b = hb_.next()
            y_t, y_b = yb_.next()
            P.dma("sp", h_t[:, :, 0:n], hv[:, :, t0:t0 + n], writes=[h_b])
            P.dma("sp", y_t[:, :, 0:n], YTv[:, :, t0:t0 + n], writes=[y_b])
            emit_outproj_residual(S, li, y_t, y_b, wo_t, wo_b, h_t, h_b, n, t0, mi, G, ov, obufs)


MIXERS[0] = mixer_fnet


def attn_proj(nc, li, last, hin, I, scratch, tabs, wname, nq, nk, nv, rope, q_ctx):
    QH, KH, VT = scratch["QH"], scratch["KH"], scratch["VT"]
    A, Bv = tabs["A1"], tabs["B1"]
    hv, hbuf = hin
    tiles = [(t0, n, 0) for (t0, n) in col_tiles(0, TL, 256)] + [(TL, TCX, 1)]
    with Stage(nc, f"ap{li}") as S:
        P = S.P
        S.psum_pool(8, (128, 256))
        nrm = NormCtx(S, w=256, nh=2, nb=2)
        ncol = (nq + nk) * 128 + nv
        w_t, w_b = S.sb("w", [128, KC, ncol], BF16)
        wv_ = I[wname].rearrange("(kc p) f -> p kc f", p=128)
        for c0 in range(0, ncol, 512):
            c1 = min(ncol, c0 + 512)
            P.dma("pool", w_t[:, :, c0:c1], wv_[:, :, c0:c1], writes=[w_b])
        abufs = RR(S.sbs("a", [128, KC, 256], BF16, 2))
        qo = RR(S.sbs("qo", [128, 256], BF16, 4))
        vo = RR(S.sbs("vo", [128, 256], BF16, 4))
        if rope:
            rot_t, rot_b = S.sb("rot", [128, 128], BF16)
            bd_t, bd_b = S.sb("bd", [128, 128], BF16)
            g_t, g_b = S.sb("g", [128, 2], F32)
            cos_t, cos_b = S.sb("cos", [128, T], F32)
            sin_t, sin_b = S.sb("sin", [128, T], F32)
            P.dma("sp", rot_t[:], I["rotT"], writes=[rot_b])
            P.dma("sp", bd_t[:], I["bd64"], writes=[bd_b])
            P.dma("sp", g_t[:], I["gqk"], writes=[g_b])
            P.dma("sp", cos_t[:], I["cosF"], writes=[cos_b])
            P.dma("act", sin_t[:], I["sinF"], writes=[sin_b])
            qf = RR(S.sbs("qf", [128, 256], F32, 6))
            sq = RR(S.sbs("sq", [128, 256], BF16, 4))
            rs = RR(S.sbs("rs", [128, 256], F32, 4))
            qn = RR(S.sbs("qn", [128, 256], BF16, 6))
            t1 = RR(S.sbs("t1", [128, 256], F32, 3))
            t2 = RR(S.sbs("t2", [128, 256], F32, 3))
        def proj_mm(ps_t, ps_b, a_t, a_b, c, n):
            for kc in range(KC):
                P.op("pe", lambda e, kc=kc: e.matmul(
                    ps_t[:, 0:n], w_t[:, kc, c * 128:(c + 1) * 128], a_t[:, kc, 0:n], start=(kc == 0), stop=(kc == KC - 1)),
                    reads=[w_b, a_b], writes=[ps_b])

        def store(o_t, o_b, c, t0, n):
            isq = c < nq
            dst = QH if isq else KH
            hc = c if isq else c - nq
            P.dma("sp", dst[2 * hc:2 * hc + 2, :, t0:t0 + n].rearrange("h d t -> (h d) t"), o_t[:, 0:n], reads=[o_b], writes=[Buf("qk")])

        def vproj(a_t, a_b, t0, n):
            for q in range(n // 128):
                for vc in range(0, nv, 256):
                    ps_t, ps_b = S.ps()
                    for kc in range(KC):
                        P.op("pe", lambda e, kc=kc: e.matmul(
                            ps_t[:, :], a_t[:, kc, q * 128:(q + 1) * 128], w_t[:, kc, (nq + nk) * 128 + vc:(nq + nk) * 128 + vc + 256],
                            start=(kc == 0), stop=(kc == KC - 1)), reads=[w_b, a_b], writes=[ps_b])
                    v_t, v_b = vo.next()
                    P.op("act", lambda e: e.copy(v_t[:, :], ps_t[:, :]), reads=[ps_b], writes=[v_b])
                    r0 = t0 + q * 128
                    P.dma("act", VT[r0:r0 + 128, vc:vc + 256], v_t[:, :], reads=[v_b], writes=[Buf("v")])

        if not rope:
            def norm_of(ti):
                t0_, n_, mi_ = tiles[ti]
                ab_ = abufs.next()
                nrm.run((hv, hbuf), t0_, n_, ab_[0], ab_[1], 0, A[:, li], Bv[:, li], mi_)
                return ab_

            a_next = norm_of(0)
            for ti, (t0, n, mi) in enumerate(tiles):
                a_t, a_b = a_next
                for c in range(nq + nk):
                    if c == 6 and ti + 1 < len(tiles):
                        a_next = norm_of(ti + 1)
                    if c < nq and mi == 1 and not q_ctx:
                        continue
                    ps_t, ps_b = S.ps()
                    proj_mm(ps_t, ps_b, a_t, a_b, c, n)
                    o_t, o_b = qo.next()
                    if c % 2 == 0:
                        P.op("act", lambda e: e.copy(o_t[:, 0:n], ps_t[:, 0:n]), reads=[ps_b], writes=[o_b])
                    else:
                        P.op("dve", lambda e: e.tensor_copy(o_t[:, 0:n], ps_t[:, 0:n]), reads=[ps_b], writes=[o_b])
                    store(o_t, o_b, c, t0, n)
                vproj(a_t, a_b, t0, n)
        else:
            units = []
            for ti, (t0, n, mi) in enumerate(tiles):
                k = 0
                for c in range(nq + nk):
                    if c < nq and mi == 1 and not q_ctx:
                        continue
                    units.append(dict(t0=t0, n=n, mi=mi, c=c, ti=ti, k=k))
                    k += 1
                units[-1]["lastc"] = True
            NU = len(units)
            tile_a = {}

            def norm_of(ti):
                if ti not in tile_a and ti < len(tiles):
                    t0_, n_, mi_ = tiles[ti]
                    ab_ = abufs.next()
                    nrm.run((hv, hbuf), t0_, n_, ab_[0], ab_[1], 0, A[:, li], Bv[:, li], mi_)
                    tile_a[ti] = ab_

            def st0(u):
                t0, n, c = u["t0"], u["n"], u["c"]
                norm_of(u["ti"])
                if u["k"] == 4:
                    norm_of(u["ti"] + 1)
                a_t, a_b = tile_a[u["ti"]]
                ps_t, ps_b = S.ps()
                proj_mm(ps_t, ps_b, a_t, a_b, c, n)
                u["qf"] = qf.next()
                u["sq"] = sq.next()
                qf_t, qf_b = u["qf"]
                sq_t, sq_b = u["sq"]
                P.op("act", lambda e: e.copy(qf_t[:, 0:n], ps_t[:, 0:n]), reads=[ps_b], writes=[qf_b])
                P.op("act", lambda e: e.activation(sq_t[:, 0:n], ps_t[:, 0:n], AF.Square), reads=[ps_b], writes=[sq_b])
                if u.get("lastc"):
                    vproj(a_t, a_b, t0, n)

            def st1(u):
                n, c = u["n"], u["c"]
                gi = 0 if c < nq else 1
                qf_t, qf_b = u["qf"]
                sq_t, sq_b = u["sq"]
                rs_t, rs_b = rs.next()
                u["qn"] = qn.next()
                qn_t, qn_b = u["qn"]
                p2_t, p2_b = S.ps()
                P.op("pe", lambda e: e.matmul(p2_t[:, 0:n], bd_t[:], sq_t[:, 0:n], start=True, stop=True), reads=[sq_b, bd_b], writes=[p2_b])
                P.op("act", lambda e: e.activation(rs_t[:, 0:n], p2_t[:, 0:n], AF.Sqrt, bias=EPS, scale=1.0 / 64), reads=[p2_b], writes=[rs_b])
                P.op("dve", lambda e: e.reciprocal(rs_t[:, 0:n], rs_t[:, 0:n]), reads=[rs_b], writes=[rs_b])
                P.op("dve", lambda e: e.scalar_tensor_tensor(qn_t[:, 0:n], qf_t[:, 0:n], g_t[:, gi:gi + 1], rs_t[:, 0:n], ALU.mult, ALU.mult),
                     reads=[qf_b, rs_b, g_b], writes=[qn_b])

            def st2(u):
                t0, n, c = u["t0"], u["n"], u["c"]
                qn_t, qn_b = u["qn"]
                t1_t, t1_b = t1.next()
                t2_t, t2_b = t2.next()
                o_t, o_b = qo.next()
                p3_t, p3_b = S.ps()
                P.op("pe", lambda e: e.matmul(p3_t[:, 0:n], rot_t[:], qn_t[:, 0:n], start=True, stop=True), reads=[qn_b, rot_b], writes=[p3_b])
                P.op("dve", lambda e: e.tensor_tensor(t1_t[:, 0:n], qn_t[:, 0:n], cos_t[:, t0:t0 + n], ALU.mult), reads=[qn_b, cos_b], writes=[t1_b])
                P.op("dve", lambda e: e.tensor_tensor(t2_t[:, 0:n], p3_t[:, 0:n], sin_t[:, t0:t0 + n], ALU.mult), reads=[p3_b, sin_b], writes=[t2_b])
                P.op("pool", lambda e: e.tensor_tensor(o_t[:, 0:n], t1_t[:, 0:n], t2_t[:, 0:n], ALU.add), reads=[t1_b, t2_b], writes=[o_b])
                store(o_t, o_b, c, t0, n)

            for i in range(NU + 4):
                if i < NU:
                    st0(units[i])
                if 0 <= i - 2 < NU:
                    st1(units[i - 2])
                if 0 <= i - 4 < NU:
                    st2(units[i - 4])


def attn_out(nc, li, last, hin, hmid, I, scratch, tabs, wname):
    OH = scratch["OH"]
    G = tabs["G1"]
    hv, hbuf = hin
    ov, _ = hmid
    tiles = [(t0, n, 0) for (t0, n) in col_tiles(0, TL, 256)] + ([] if last else [(TL, TCX, 1)])
    with Stage(nc, f"ao{li}") as S:
        P = S.P
        S.psum_pool(8, (128, 256))
        wo_t, wo_b = S.sb("wo", [128, KC, D], BF16)
        P.dma("pool", wo_t[:], I[wname].rearrange("(kc p) f -> p kc f", p=128), writes=[wo_b])
        hb_ = RR(S.sbs("h", [128, KC, 256], F32, 2))
        yb_ = RR(S.sbs("y", [128, KC, 256], BF16, 2))
        obufs = RR(S.sbs("o", [128, 256], F32, 4))
        OHv = OH.rearrange("(c two) d t -> (two d) c t", two=2)
        for (t0, n, mi) in tiles:
            h_t, h_b = hb_.next()
            y_t, y_b = yb_.next()
            P.dma("sp", h_t[:, :, 0:n], hv[:, :, t0:t0 + n], writes=[h_b])
            P.dma("sp", y_t[:, :, 0:n], OHv[:, :, t0:t0 + n], writes=[y_b])
            emit_outproj_residual(S, li, y_t, y_b, wo_t, wo_b, h_t, h_b, n, t0, mi, G, ov, obufs)


def attn_pipeline(S, items, spp, pbs, tmps, LA=3):
    P = S.P
    n = len(items)
    sbanks = [None] * n

    def issue_s(i):
        it = items[i]
        if it.get("pre") is not None:
            it["pre"]()
        s_t, s_b = spp.next()
        sbanks[i] = (s_t, s_b)
        mk, ncol = it["mk"], it["ncol"]
        (k_ap, k_b), (q_ap, q_b) = it["k"], it["q"]
        if it["bias"] is None:
            P.op("pe", lambda e: e.matmul(s_t[0:mk, 0:ncol], k_ap, q_ap, start=True, stop=True), reads=[k_b, q_b], writes=[s_b])
        else:
            (b_ap, b_b), (i_ap, i_b) = it["bias"], it["ident"]
            P.op("pe", lambda e: e.matmul(s_t[0:mk, 0:ncol], k_ap, q_ap, start=True, stop=False), reads=[k_b, q_b], writes=[s_b])
            P.op("pe", lambda e: e.matmul(s_t[0:mk, 0:ncol], i_ap, b_ap, start=False, stop=True), reads=[b_b, i_b], writes=[s_b])

    for i in range(min(LA, n)):
        issue_s(i)
    for i in range(n):
        if i + LA < n:
            issue_s(i + LA)
        it = items[i]
        s_t, s_b = sbanks[i]
        mk, ncol, c0 = it["mk"], it["ncol"], it["c0"]
        p_t, p_b = pbs.next()
        P.op("act", lambda e: e.activation(p_t[0:mk, 0:ncol], s_t[0:mk, 0:ncol], AF.Exp, scale=0.125), reads=[s_b], writes=[p_b])
        (v_ap, v_b), (acc_t, acc_b) = it["v"], it["acc"]
        first, lastw = it["first"], it["last"]
        kp = it.get("kp", mk)
        P.op("pe", lambda e: e.matmul(acc_t[:, c0:c0 + ncol], v_ap, p_t[0:kp, 0:ncol], start=first, stop=lastw, skip_group_check=True),
             reads=[v_b, p_b], writes=[acc_b])
        if it["fin"] is not None:
            it["fin"]()


def gqa_pipeline(S, items, spp, pbs, LA=2):
    P = S.P
    assert len(items) % 2 == 0
    ng = len(items) // 2
    sb = [None] * ng

    def issue_s(g):
        s_t, s_b = spp.next()
        sb[g] = (s_t, s_b)
        for j in range(2):
            it = items[2 * g + j]
            if it.get("pre") is not None:
                it["pre"]()
            (k_ap, k_b), (q_ap, q_b) = it["k"], it["q"]
            P.op("pe", lambda e: e.matmul(s_t[:, j * 512:(j + 1) * 512], k_ap, q_ap, start=True, stop=True), reads=[k_b, q_b], writes=[s_b])

    for g in range(min(LA, ng)):
        issue_s(g)
    for g in range(ng):
        if g + LA < ng:
            issue_s(g + LA)
        s_t, s_b = sb[g]
        p_t, p_b = pbs.next()
        P.op("act", lambda e: e.activation(p_t[:, :], s_t[:, :], AF.Exp, scale=0.125), reads=[s_b], writes=[p_b])
        for j in range(2):
            it = items[2 * g + j]
            (v_ap, v_b), (acc_t, acc_b) = it["v"], it["acc"]
            first, lastw = it["first"], it["last"]
            P.op("pe", lambda e: e.matmul(acc_t, v_ap, p_t[:, j * 512:(j + 1) * 512], start=first, stop=lastw, skip_group_check=True),
                 reads=[v_b, p_b], writes=[acc_b])
            if it["fin"] is not None:
                it["fin"]()


def attn_finish(S, acc, ncols, rec, obuf, dst_ap):
    P = S.P
    acc_t, acc_b = acc
    r_t, r_b = rec.next()
    ob_t, ob_b = obuf.next()
    P.op("dve", lambda e: e.reciprocal(r_t[0:64, 0:ncols], acc_t[64:128, 0:ncols]), reads=[acc_b], writes=[r_b])
    P.op("dve", lambda e: e.tensor_tensor(ob_t[0:64, 0:ncols], acc_t[0:64, 0:ncols], r_t[0:64, 0:ncols], ALU.mult),
         reads=[acc_b, r_b], writes=[ob_b])
    P.dma("sp", dst_ap, ob_t[0:64, 0:ncols], reads=[ob_b], writes=[Buf("oh")])


def mixer_gqa(nc, li, last, hin, hmid, I, scratch, tabs):
    QH, KH, VT, OH = scratch["QH"], scratch["KH"], scratch["VT"], scratch["OH"]
    attn_proj(nc, li, last, hin, I, scratch, tabs, "w_att_qkv", 8, 2, 256, True, False)
    with Stage(nc, f"ga{li}") as S:
        P = S.P
        S.psum_pool(4, (128, 1024))
        acc_t0 = S.psums[0][0]
        accp = RR([(acc_t0[:, 0:512], Buf("accA")), (acc_t0[:, 512:1024], Buf("accB"))])
        spp = RR(S.psums[1:4])
        NKT = T // 128
        kts = RR(S.sbs("kt", [128, T], BF16, 2))
        vts = RR(S.sbs("vt", [128, NKT, 128], BF16, 2))
        for (v_t, v_b) in vts.items:
            P.op("pool", lambda e: e.memset(v_t[:, :, 64:128], 1.0), writes=[v_b])
        qbs = RR(S.sbs("q", [128, 512], BF16, 3))
        for (z_t, z_b) in kts.items + qbs.items:
            P.op("pool", lambda e: e.memset(z_t[64:128, :], 0.0), writes=[z_b])
        pbs = RR(S.sbs("p", [128, 1024], BF16, 3))
        rec = RR(S.sbs("rec", [64, 512], F32, 2))
        obuf = RR(S.sbs("ob", [64, 512], BF16, 2))
        items = []
        tiles_q = []
        for kh in range(4):
            kt_t, kt_b = kts.next()
            vt_t, vt_b = vts.next()
            for qt in range(TL // 128):
                tiles_q.append((kh, qt, kt_t, kt_b, vt_t, vt_b, qbs.next(), accp.next()))

        def make_load(ti):
            kh, qt, kt_t, kt_b, vt_t, vt_b, (q_t, q_b), acc = tiles_q[ti]

            def load():
                if qt == 0:
                    P.dma("sp", kt_t[0:64, :], KH[kh], writes=[kt_b])
                    VTv = VT[:, kh * 64:(kh + 1) * 64].rearrange("(n p) d -> p n d", p=128)
                    for a in range(0, NKT, 17):
                        P.dma("act", vt_t[:, a:a + 17, 0:64], VTv[:, a:a + 17, :], writes=[vt_b])
                qsrc = QH[4 * kh:4 * kh + 4, :, qt * 128:(qt + 1) * 128].rearrange("h d t -> d h t")
                P.dma("sp", q_t[0:64, :].rearrange("d (h t) -> d h t", h=4), qsrc, writes=[q_b])
            return load

        for ti, (kh, qt, kt_t, kt_b, vt_t, vt_b, (q_t, q_b), acc) in enumerate(tiles_q):
            dst = OH[4 * kh:4 * kh + 4, :, qt * 128:(qt + 1) * 128].rearrange("h d t -> d h t")

            def fin(acc=acc, dst=dst):
                attn_finish(S, acc, 512, rec, obuf, dst)

            def pre(ti=ti):
                if ti == 0:
                    make_load(0)()
                if ti + 1 < len(tiles_q):
                    make_load(ti + 1)()

            for kt in range(NKT):
                items.append(dict(k=(kt_t[:, kt * 128:(kt + 1) * 128], kt_b), q=(q_t[:, :], q_b), mk=128, ncol=512, bias=None,
                                  v=(vt_t[:, kt, :], vt_b), acc=acc, c0=0, first=(kt == 0), last=(kt == NKT - 1),
                                  fin=fin if kt == NKT - 1 else None, pre=pre if kt == 0 else None))
        gqa_pipeline(S, items, spp, pbs, LA=2)
    attn_out(nc, li, last, hin, hmid, I, scratch, tabs, "w_att_out")


MIXERS[3] = mixer_gqa


def na_rowsets():
    rs = lambda r: min(max(r - 4, 0), 56)
    out = []
    for j in range(8):
        items = []
        for kr in range(64):
            qs = [qr for qr in range(8 * j, 8 * j + 8) if rs(qr) <= kr < rs(qr) + 8]
            if qs:
                assert qs == list(range(qs[0], qs[-1] + 1))
                items.append((kr, qs[0], qs[-1] + 1))
        out.append(items)
    return out


def mixer_na(nc, li, last, hin, hmid, I, scratch, tabs):
    QH, KH, VT, OH = scratch["QH"], scratch["KH"], scratch["VT"], scratch["OH"]
    attn_proj(nc, li, last, hin, I, scratch, tabs, "w_na_qkv", 8, 8, D, False, True)
    rowsets = na_rowsets()
    with Stage(nc, f"na{li}") as S:
        P = S.P
        S.psum_pool(8)
        accp = RR(S.psums[0:2])
        spp = RR(S.psums[2:8])
        NR = T // 64
        kts = RR(S.sbs("kt", [128, T], BF16, 2))
        qhs = RR(S.sbs("qh", [128, T], BF16, 2))
        vts = RR(S.sbs("vt", [128, NR, 128], BF16, 2))
        for (v_t, v_b) in vts.items:
            P.op("pool", lambda e: e.memset(v_t[0:64, :, 64:128], 1.0), writes=[v_b])
        tts = RR(S.sbs("tt", [64, 15 * 64], F32, 2))
        tt8s = RR(S.sbs("tt8", [128, 15 * 64], BF16, 2))
        id_t, id_b = S.sb("ident", [128, 64], BF16)
        P.dma("sp", id_t[0:64, :], I["ident64"], writes=[id_b])
        pbs = RR(S.sbs("p", [128, 512], BF16, 4))
        for (z_t, z_b) in kts.items + qhs.items + tt8s.items + pbs.items + [(id_t, id_b)]:
            P.op("pool", lambda e: e.memset(z_t[64:128, :], 0.0), writes=[z_b])
        for (v_t, v_b) in vts.items:
            P.op("pool", lambda e: e.memset(v_t[64:128, :, :], 0.0), writes=[v_b])
        rec = RR(S.sbs("rec", [64, 512], F32, 2))
        obuf = RR(S.sbs("ob", [64, 512], BF16, 2))
        items = []
        for h in range(16):
            kt_t, kt_b = kts.next()
            qh_t, qh_b = qhs.next()
            vt_t, vt_b = vts.next()
            tt_t, tt_b = tts.next()
            t8_t, t8_b = tt8s.next()
            VTv = VT[:, h * 64:(h + 1) * 64].rearrange("(r p) d -> p r d", p=64)

            def load(h=h, kt_t=kt_t, kt_b=kt_b, qh_t=qh_t, qh_b=qh_b, vt_t=vt_t, vt_b=vt_b, tt_t=tt_t, tt_b=tt_b, VTv=VTv,
                     t8_t=t8_t, t8_b=t8_b):
                P.dma("sp", kt_t[0:64, :], KH[h], writes=[kt_b])
                P.dma("sp", qh_t[0:64, :], QH[h], writes=[qh_b])
                P.dma("sp", tt_t[:], I["nab"][h], writes=[tt_b])
                P.op("dve", lambda e: e.tensor_scalar(t8_t[0:64, :], tt_t[:, :], 8.0, None, ALU.mult), reads=[tt_b], writes=[t8_b])
                for a in range(0, NR, 17):
                    P.dma("act", vt_t[0:64, a:a + 17, 0:64], VTv[:, a:a + 17, :], writes=[vt_b])

            bands = [(512 * j, 512, rowsets[j]) for j in range(8)] + ([] if last else [(TL, TCX, [])])
            for bi, (q0, nq_, rows) in enumerate(bands):
                acc = accp.next()
                work = [(64 + i, 0, nq_, None) for i in range(4)]
                for (kr, qa, qb) in rows:
                    work.append((kr, (qa * 64) - q0, (qb * 64) - q0, (qa - kr + 7) * 64))

                def fin(acc=acc, h=h, q0=q0, nq_=nq_):
                    attn_finish(S, acc, nq_, rec, obuf, OH[h, :, q0:q0 + nq_])

                for wi, (kr, c0, c1, boff) in enumerate(work):
                    ncol = c1 - c0
                    items.append(dict(k=(kt_t[:, kr * 64:(kr + 1) * 64], kt_b), q=(qh_t[:, q0 + c0:q0 + c1], qh_b), mk=64, ncol=ncol,
                                      bias=None if boff is None else (t8_t[:, boff:boff + ncol], t8_b), ident=(id_t[:, :], id_b),
                                      v=(vt_t[:, kr, :], vt_b), kp=128, acc=acc, c0=c0, first=(wi == 0), last=(wi == len(work) - 1),
                                      fin=fin if wi == len(work) - 1 else None, pre=load if (bi == 0 and wi == 0) else None))
        attn_pipeline(S, items, spp, pbs, None, LA=4)
    attn_out(nc, li, last, hin, hmid, I, scratch, tabs, "w_na_out")


MIXERS[1] = mixer_na
SEMS = [None]
DEBUG_OUT = [False]
SCOPES = [False]


def build(depth=DEPTH, mixers=(0, 1, 2, 3), do_final=True):
    nc = bass.Bass("TRN2", target_bir_lowering=False)
    dt = lambda name, shape, dtype=F32, kind="ExternalInput": nc.dram_tensor(name, list(shape), dtype, kind=kind).ap()
    I = {}
    I["xT"] = dt("xT", [D, T])
    I["condT"] = dt("condT", [128, KC, 2])
    I["w_mod"] = dt("w_mod", [DEPTH, D, 6 * D])
    I["bmod2"] = dt("bmod2", [2, DEPTH, 6 * D])
    I["ident2"] = dt("ident2", [2, 2])
    I["gmixT"] = dt("gmixT", [128, DEPTH, KC])
    I["gffnT"] = dt("gffnT", [128, DEPTH, KC])
    I["gfinT"] = dt("gfinT", [128, KC, 1])
    I["w_ffn_up"] = dt("w_ffn_up", [DEPTH, D, 2 * FF])
    I["w_ffn_down"] = dt("w_ffn_down", [DEPTH, FF, D])
    I["convT"] = dt("convT", [DEPTH, 128, 4, FC])
    for name, shape, dtype in EXTRA_INPUTS:
        I[name] = dt(name, shape, dtype)
    outT = dt("outT", [D, TL], F32, "ExternalOutput")
    hA = dt("hA", [D, T], F32, "Internal")
    hB = dt("hB", [D, T], F32, "Internal")
    scratch = {name: dt(name, shape, dtype, "ExternalOutput" if DEBUG_OUT[0] else "Internal") for name, shape, dtype in SCRATCH}
    with ExitStack() as st:
        SEMS[0] = Sems(nc, st)
        tabs = {k: st.enter_context(nc.sbuf_tensor(f"tab_{k}", [128, DEPTH, KC, 2], F32))
                for k in ("A1", "B1", "G1", "A2", "B2", "G2")}
        tabs["buf"] = Buf("tabs")
        with Stage(nc, "ada") as S:
            stage_ada(nc, S, I["condT"], I["w_mod"], I["bmod2"], I["gmixT"], I["gffnT"], I["ident2"], tabs)
        hin = (hview(I["xT"]), Buf("xT"))
        vA, vB = hview(hA), hview(hB)
        for li in range(depth):
            last = li == DEPTH - 1
            hmid = (vB, Buf("hB"))
            hout = (vA, Buf("hA"))
            mx = MIXERS.get(mixers[li]) if mixers[li] is not None else None
            if mx is None:
                with Stage(nc, f"mx{li}") as S:
                    stage_copy_mixer(nc, S, hin, hmid, last)
            else:
                mx(nc, li, last, hin, hmid, I, scratch, tabs)
            with Stage(nc, f"ffn{li}") as S:
                stage_ffn(nc, S, li, last, hmid, hout, I["w_ffn_up"], I["w_ffn_down"], I["convT"], tabs)
            hin = (vA, Buf("hA"))
        if do_final:
            with Stage(nc, "final") as S:
                stage_final(nc, S, hin, outT, I["gfinT"])
        else:
            with Stage(nc, "dump") as S:
                ov = outT.rearrange("(kc p) t -> p kc t", p=128)
                for (t0, n) in col_tiles(0, TL, 1024):
                    S.P.dma("sp", ov[:, :, t0:t0 + n], hin[0][:, :, t0:t0 + n], writes=[Buf("x")])
    return nc


def cols(v):
    v = np.asarray(v, np.float32)
    lead = v.shape[:-1]
    n = v.shape[-1] // 128
    r = v.reshape(*lead, n, 128)
    return np.ascontiguousarray(np.moveaxis(r, -1, 0))


_CONSTS = {}


def dft_consts():
    if not _CONSTS:
        bf = ml_dtypes.bfloat16
        n = np.arange(256, dtype=np.int64)
        ang = 2.0 * np.pi * ((n[:, None] * n[None, :]) % 256).astype(np.float64) / 256.0
        c, s_ = np.cos(ang) / 16.0, np.sin(ang) / 16.0
        _CONSTS["cs256"] = np.ascontiguousarray(np.concatenate([c, s_], 1).astype(np.float32).astype(bf))
        _CONSTS["csn256"] = np.ascontiguousarray(np.concatenate([c, -s_], 1).astype(np.float32).astype(bf))
        n = np.arange(TL, dtype=np.int64)
        ang = 2.0 * np.pi * ((n[:, None] * n[None, :]) % TL).astype(np.float64) / TL
        _CONSTS["dftC"] = np.ascontiguousarray((np.cos(ang) / 64.0).astype(np.float32).astype(bf))
        _CONSTS["dftSn"] = np.ascontiguousarray((-np.sin(ang) / 64.0).astype(np.float32).astype(bf))
    return _CONSTS


def rope_consts():
    if "rotT" not in _CONSTS:
        bf = ml_dtypes.bfloat16
        rot = np.zeros((128, 128), np.float32)
        for hh in range(2):
            for m_ in range(32):
                rot[hh * 64 + m_ + 32, hh * 64 + m_] = -1.0
                rot[hh * 64 + m_, hh * 64 + m_ + 32] = 1.0
        _CONSTS["rotT"] = rot.astype(bf)
        _CONSTS["ident64"] = np.eye(64, dtype=np.float32).astype(bf)
        bd = np.zeros((128, 128), np.float32)
        bd[0:64, 0:64] = 1.0
        bd[64:128, 64:128] = 1.0
        _CONSTS["bd64"] = bd.astype(bf)
        t = np.arange(TL)
        row = (t // 64).astype(np.float32)
        col = (t % 64).astype(np.float32)
        inv = (np.float32(10000.0) ** (-np.arange(16, dtype=np.float32) / np.float32(16))).astype(np.float32)
        ang = np.concatenate([row[:, None] * inv, col[:, None] * inv], -1).astype(np.float32)
        cos = np.cos(ang).astype(np.float32).T
        sin = np.sin(ang).astype(np.float32).T
        cosF = np.ones((128, T), np.float32)
        sinF = np.zeros((128, T), np.float32)
        for r in range(4):
            cosF[r * 32:(r + 1) * 32, :TL] = cos
            sinF[r * 32:(r + 1) * 32, :TL] = sin
        _CONSTS["cosF"], _CONSTS["sinF"] = cosF, sinF
    return {k: _CONSTS[k] for k in ("rotT", "ident64", "bd64", "cosF", "sinF")}


def na_bias_table(rpb):
    kc = np.arange(64)[:, None]
    qc = np.arange(64)[None, :]
    cs = np.clip(qc - 8, 0, 48)
    valid = (kc >= cs) & (kc < cs + 16)
    coff = np.clip(kc - qc + 15, 0, 30)
    out = np.full((16, 64, 15, 64), -30000.0, np.float32)
    for e in range(15):
        g = rpb[:, 14 - e][:, coff]
        out[:, :, e, :] = np.where(valid[None], g, np.float32(-30000.0))
    return np.ascontiguousarray(out.reshape(16, 64, 15 * 64))


def prep_inputs(inp, b):
    m = {}
    m["xT"] = np.ascontiguousarray(np.concatenate([inp["x"][b], inp["ctx"][b]], axis=0).T)
    m["condT"] = np.ascontiguousarray(cols(np.stack([inp["c"][b], inp["c_ctx"]], 0)).transpose(0, 2, 1))
    m["w_mod"] = inp["w_mod"]
    m["bmod2"] = np.ascontiguousarray(np.broadcast_to(inp["b_mod"][None].astype(np.float32), (2, DEPTH, 6 * D)))
    m["ident2"] = np.eye(2, dtype=np.float32)
    m["gmixT"] = cols(inp["g_norm_mix"])
    m["gffnT"] = cols(inp["g_norm_ffn"])
    m["gfinT"] = cols(inp["g_final"])[:, :, None].copy()
    m["w_ffn_up"] = inp["w_ffn_up"]
    m["w_ffn_down"] = inp["w_ffn_down"]
    cv = np.concatenate([inp["w_ffn_conv"], inp["b_ffn_conv"][:, None, :]], axis=1)
    m["convT"] = np.ascontiguousarray(cols(cv).transpose(1, 0, 2, 3))
    m.update(dft_consts())
    m["w_fnet_out"] = inp["w_fnet_out"][0]
    m.update(rope_consts())
    m["w_na_qkv"] = inp["w_na_qkv"][0]
    m["w_na_out"] = inp["w_na_out"][0]
    m["nab"] = na_bias_table(inp["na_rel_bias"][0])
    m["w_att_qkv"] = inp["w_att_qkv"][0]
    m["w_att_out"] = inp["w_att_out"][0]
    m["gqk"] = np.ascontiguousarray(np.stack([np.tile(inp["g_att_q"][0], 2), np.tile(inp["g_att_k"][0], 2)], 1).astype(np.float32))
    m["w_sg_in"] = inp["w_sg_in"][0]
    m["w_sg_out"] = inp["w_sg_out"][0]
    m["gsgT"] = cols(inp["g_sg_v"][0])
    m["wsT"] = np.ascontiguousarray(inp["w_sg_spatial"][0].transpose(2, 0, 1))
    m["bsR"] = np.ascontiguousarray(np.broadcast_to(inp["b_sg_spatial"][0][None], (128, 4, 128)))
    return m


REAL_CORES = (0, 1, 4, 5)


def kernel(**inp):
    inp = {k: np.asarray(v) for k, v in inp.items()}
    nc = build()
    in_maps = [None] * 8
    for b, core in enumerate(REAL_CORES):
        in_maps[core] = prep_inputs(inp, b)
    zeros = {k: np.zeros_like(v) for k, v in in_maps[REAL_CORES[0]].items()}
    for core in range(8):
        if in_maps[core] is None:
            in_maps[core] = zeros
    res = run_bass_kernel_spmd(nc, in_maps, core_ids=list(range(8)))
    out = np.stack([res.results[core]["outT"].T for core in REAL_CORES], 0)
    return np.ascontiguousarray(out.astype(np.float32))
```

```python
import numpy as np
import ml_dtypes
import concourse.bass as bass
import concourse.mybir as mybir
from concourse.bass_utils import run_bass_kernel_spmd
from contextlib import ExitStack

F32 = mybir.dt.float32
BF16 = mybir.dt.bfloat16
AF = mybir.ActivationFunctionType
ALU = mybir.AluOpType
AX = mybir.AxisListType

ENGINES = ("pe", "act", "dve", "pool", "sp")
EPOCH = 20000
N_EPOCH_SEMS = 6
DMA_POOL = 12


import types


def freeze(fn):
    if fn.__closure__ is None:
        return fn
    cells = []
    for c in fn.__closure__:
        try:
            cells.append(types.CellType(c.cell_contents))
        except ValueError:
            cells.append(c)
    return types.FunctionType(fn.__code__, fn.__globals__, fn.__name__, fn.__defaults__, tuple(cells))


class Buf:
    __slots__ = ("name", "w", "r")

    def __init__(self, name):
        self.name = name
        self.w = None
        self.r = []


class Op:
    __slots__ = ("eng", "fn", "is_dma", "deps", "inc", "cidx", "dslot", "dval", "gid")


class Sems:
    def __init__(self, nc, st):
        self.csem = {e: [st.enter_context(nc.semaphore(f"c_{e}_{k}")) for k in range(N_EPOCH_SEMS)]
                     for e in ENGINES if e != "sp"}
        self.dsem = {e: [st.enter_context(nc.semaphore(f"d_{e}_{k}")) for k in range(DMA_POOL)]
                     for e in ("sp", "act", "pool")}
        self.cbase = {e: 0 for e in ENGINES}
        self.ndma = {e: 0 for e in ENGINES}


class Prog:
    def __init__(self, nc, sems):
        self.nc = nc
        self.sems = sems
        self.ops = []
        self.streams = {e: [] for e in ENGINES}
        self.ndma = sems.ndma
        self.barrier_deps = {}
        self.dma_ops = []

    def _add(self, eng, fn, reads, writes, is_dma):
        op = Op()
        op.eng, op.fn, op.is_dma = eng, fn, is_dma
        op.inc = False
        op.cidx = None
        op.gid = len(self.ops)
        deps = set()
        for b in reads:
            if b.w is not None:
                deps.add(b.w)
        for b in writes:
            if b.w is not None:
                deps.add(b.w)
            for r in b.r:
                deps.add(r)
        deps.discard(op)
        deps.update(self.barrier_deps.pop(eng, ()))
        keep = {}
        op.deps = []
        for d in deps:
            if d.is_dma:
                op.deps.append(d)
            elif d.eng == "pe" and eng == "pe" and not is_dma:
                continue
            elif d.eng not in keep or keep[d.eng].gid < d.gid:
                keep[d.eng] = d
        op.deps.extend(keep.values())
        for d in op.deps:
            d.inc = True
        for b in reads:
            b.r.append(op)
        for b in writes:
            b.w = op
            b.r = []
        if is_dma:
            j = self.ndma[eng]
            self.ndma[eng] += 1
            op.dslot = j % DMA_POOL
            op.dval = 16 * (j // DMA_POOL + 1)
            self.dma_ops.append(op)
        self.ops.append(op)
        self.streams[eng].append(op)
        return op

    def op(self, eng, fn, reads=(), writes=()):
        return self._add(eng, freeze(fn), list(reads), list(writes), False)

    def dma(self, eng, out, in_, reads=(), writes=()):
        def fn(e):
            return e.dma_start(out=out, in_=in_)
        return self._add(eng, fn, list(reads), list(writes), True)

    def emit(self, final_wait_ops=()):
        nc = self.nc
        with ExitStack() as st:
            csem, dsem = self.sems.csem, self.sems.dsem
            for e in ENGINES:
                k = self.sems.cbase[e]
                for op in self.streams[e]:
                    if not op.is_dma and op.inc:
                        k += 1
                        op.cidx = k
                self.sems.cbase[e] = k
                assert k < EPOCH * N_EPOCH_SEMS, (e, k)
            block = st.enter_context(nc.Block(no_gpsimd_drain=True))

            def run_stream(ename, e):
                seen_c = {}
                seen_d = {}
                last_on_slot = {}

                def wait_for(d):
                    if d.is_dma:
                        key = (d.eng, d.dslot)
                        if seen_d.get(key, 0) >= d.dval:
                            return
                        seen_d[key] = d.dval
                        e.wait_ge(dsem[d.eng][d.dslot], d.dval)
                    else:
                        if seen_c.get(d.eng, 0) >= d.cidx:
                            return
                        seen_c[d.eng] = d.cidx
                        ep, v = divmod(d.cidx - 1, EPOCH)
                        e.wait_ge(csem[d.eng][ep], v + 1)

                for op in self.streams[ename]:
                    for d in sorted(op.deps, key=lambda o: o.gid):
                        wait_for(d)
                    if op.is_dma:
                        prev = last_on_slot.get(op.dslot)
                        if prev is not None:
                            wait_for(prev)
                        last_on_slot[op.dslot] = op
                        ins = op.fn(e)
                        ins.then_inc(dsem[ename][op.dslot], 16)
                    else:
                        ins = op.fn(e)
                        if op.inc:
                            ep, _ = divmod(op.cidx - 1, EPOCH)
                            ins.then_inc(csem[ename][ep], 1)
                if ename == "sp":
                    for d in final_wait_ops:
                        wait_for(d)

            block.sync(lambda e: run_stream("sp", e))
            block.scalar(lambda e: run_stream("act", e))
            block.vector(lambda e: run_stream("dve", e))
            block.gpsimd(lambda e: run_stream("pool", e))
            block.tensor(lambda e: run_stream("pe", e))


class Stage:
    def __init__(self, nc, name):
        self.nc, self.name = nc, name

    def __enter__(self):
        self.st = ExitStack()
        if SCOPES[0]:
            self.st.enter_context(self.nc.named_scope(self.name))
        self.P = Prog(self.nc, SEMS[0])
        self.npsum = 0
        self.psums = []
        return self

    def sb(self, name, shape, dt):
        t = self.st.enter_context(self.nc.sbuf_tensor(f"{self.name}_{name}", list(shape), dt))
        return t, Buf(name)

    def sbs(self, name, shape, dt, n):
        return [self.sb(f"{name}{i}", shape, dt) for i in range(n)]

    def psum_pool(self, n, shape=(128, 512), dt=F32):
        self.psums = [(self.st.enter_context(self.nc.psum_tensor(f"{self.name}_ps{i}", list(shape), dt)),
                       Buf(f"ps{i}")) for i in range(n)]
        self.pi = 0

    def ps(self):
        r = self.psums[self.pi % len(self.psums)]
        self.pi += 1
        return r

    def __exit__(self, et, ev, tb):
        if et is None:
            P = self.P
            tail = []
            for q in ("sp", "act", "pool"):
                tail.extend([d for d in P.dma_ops if d.eng == q][-DMA_POOL:])
            P.emit(final_wait_ops=tail)
        self.st.close()
        return False


class RR:
    def __init__(self, items):
        self.items, self.i = items, 0

    def next(self):
        r = self.items[self.i % len(self.items)]
        self.i += 1
        return r


D = 1024
KC = 8
TL = 4096
TCX = 256
T = TL + TCX
FF = 2816
FC = 22
DEPTH = 4
EPS = 1e-6


def col_tiles(lo, hi, w=512, slack=0):
    out = []
    while lo < hi:
        n = min(w, hi - lo)
        if hi - lo - n <= slack:
            n = hi - lo
        out.append((lo, n))
        lo += n
    return out


def hview(h):
    return h.rearrange("(kc p) t -> p kc t", p=128)


def emit_norm_a(S, hsb, hb, n, ssq):
    sq_t, sq_b = ssq
    for kc in range(KC):
        S.P.op("act", lambda e, kc=kc: e.activation(sq_t[:, kc, 0:n], hsb[:, kc, 0:n], AF.Square), reads=[hb], writes=[sq_b])


def emit_norm_b(S, hsb, hb, n, asb, ab, aoff, A, Bv, ones, ssq, rstd, tmp, tb, mi, tabb=(), ones_b=None):
    P = S.P
    sq_t, sq_b = ssq
    r_t, r_b = rstd
    ps_t, ps_b = S.ps()
    for kc in range(KC):
        P.op("pe", lambda e, kc=kc: e.matmul(ps_t[:, 0:n], ones[:], sq_t[:, kc, 0:n], start=(kc == 0), stop=(kc == KC - 1)),
             reads=[sq_b, ones_b], writes=[ps_b])
    P.op("act", lambda e: e.activation(r_t[:, 0:n], ps_t[:, 0:n], AF.Sqrt, bias=EPS, scale=1.0 / D), reads=[ps_b], writes=[r_b])
    P.op("dve", lambda e: e.reciprocal(r_t[:, 0:n], r_t[:, 0:n]), reads=[r_b], writes=[r_b])
    for kc in range(KC):
        P.op("dve", lambda e, kc=kc: e.scalar_tensor_tensor(tmp[:, kc, 0:n], hsb[:, kc, 0:n], A[:, kc, mi:mi + 1], r_t[:, 0:n],
                                                           ALU.mult, ALU.mult), reads=[hb, r_b, *tabb], writes=[tb[kc]])
        if Bv is None:
            continue
        P.op("act", lambda e, kc=kc: e.activation(asb[:, kc, aoff:aoff + n], tmp[:, kc, 0:n], AF.Identity,
                                                  bias=Bv[:, kc, mi:mi + 1], scale=1.0), reads=[tb[kc]], writes=[ab])


class NormCtx:
    def __init__(self, S, w=512, nh=2, nb=2):
        self.S = S
        w = w + 4
        self.w = w
        self.h = RR(S.sbs("nh", [128, KC, w], F32, nh))
        self.sq = RR(S.sbs("nsq", [128, KC, w], BF16, nb))
        self.rs = RR(S.sbs("nrs", [128, w], F32, nb))
        self.tmp = RR([(t, [Buf(f"tmp{i}_{k}") for k in range(KC)]) for i, (t, _) in
                       enumerate(S.sbs("ntmp", [128, KC, w], F32, nb))])
        self.ones, self.ones_b = S.sb("ones", [128, 128], BF16)
        S.P.op("pool", lambda e: e.memset(self.ones[:], 1.0), writes=[self.ones_b])

    def run_a(self, hsrc, t0, n):
        S = self.S
        hv, hbuf = hsrc
        (h_t, h_b) = self.h.next()
        S.P.dma("sp", h_t[:, :, 0:n], hv[:, :, t0:t0 + n], reads=[hbuf], writes=[h_b])
        ssq = self.sq.next()
        emit_norm_a(S, h_t, h_b, n, ssq)
        return (h_t, h_b, ssq, n)

    def run_b(self, st, asb, ab, aoff, A, Bv, mi, tabb=()):
        h_t, h_b, ssq, n = st
        tmp_t, tmp_b = self.tmp.next()
        emit_norm_b(self.S, h_t, h_b, n, asb, ab, aoff, A, Bv, self.ones, ssq, self.rs.next(), tmp_t, tmp_b, mi, tabb, self.ones_b)
        return h_t, h_b, tmp_t, tmp_b

    def run(self, hsrc, t0, n, asb, ab, aoff, A, Bv, mi, tabb=()):
        return self.run_b(self.run_a(hsrc, t0, n), asb, ab, aoff, A, Bv, mi, tabb)


def stage_ada(nc, S, condT, w_mod, bmod2, gmixT, gffnT, ident2, tabs):
    P = S.P
    S.psum_pool(8)
    rowp = RR(S.psums[0:4])
    trp = RR(S.psums[4:8])
    c_t, c_b = S.sb("c", [128, KC, 2], F32)
    sc_t, sc_b = S.sb("sc", [128, KC, 2], F32)
    bm_t, bm_b = S.sb("bm", [2, DEPTH, 6 * D], F32)
    gm_t, gm_b = S.sb("gm", [128, DEPTH, KC], F32)
    gf_t, gf_b = S.sb("gf", [128, DEPTH, KC], F32)
    mod_t, mod_b = S.sb("mod", [128, DEPTH, 48, 2], F32)
    id_t, id_b = S.sb("id2", [2, 2], F32)
    P.dma("sp", c_t[:], condT, writes=[c_b])
    P.dma("sp", bm_t[:], bmod2, writes=[bm_b])
    P.dma("sp", gm_t[:], gmixT, writes=[gm_b])
    P.dma("sp", gf_t[:], gffnT, writes=[gf_b])
    P.dma("sp", id_t[:], ident2, writes=[id_b])
    P.op("act", lambda e: e.activation(sc_t[:], c_t[:], AF.Silu), reads=[c_b], writes=[sc_b])
    NW = 1024
    wbufs = RR(S.sbs("w", [128, KC, NW], F32, 3))
    rows = RR(S.sbs("row", [2, NW], F32, 3))
    qi = 0
    for i in range(DEPTH):
        wv = w_mod[i].rearrange("(kc p) f -> p kc f", p=128)
        for g in range(6 * D // NW):
            w_t, w_b = wbufs.next()
            P.dma(("sp", "act")[qi % 2], w_t[:], wv[:, :, g * NW:(g + 1) * NW], writes=[w_b])
            qi += 1
            r_t, r_b = rows.next()
            for hh in range(NW // 512):
                ps_t, ps_b = rowp.next()
                for kc in range(KC):
                    P.op("pe", lambda e, kc=kc: e.matmul(ps_t[0:2, :], sc_t[:, kc, :], w_t[:, kc, hh * 512:(hh + 1) * 512],
                                                        start=(kc == 0), stop=(kc == KC - 1)), reads=[w_b, sc_b], writes=[ps_b])
                f0 = g * NW + hh * 512
                P.op("dve", lambda e: e.tensor_tensor(r_t[0:2, hh * 512:(hh + 1) * 512], ps_t[0:2, :], bm_t[0:2, i, f0:f0 + 512], ALU.add),
                     reads=[ps_b, bm_b], writes=[r_b])
            tp_t, tp_b = trp.next()
            for jj in range(NW // 128):
                P.op("pe", lambda e, jj=jj: e.matmul(tp_t[:, 2 * jj:2 * jj + 2], r_t[0:2, jj * 128:(jj + 1) * 128], id_t[0:2, 0:2],
                                                    start=True, stop=True), reads=[r_b, id_b], writes=[tp_b])
            j0 = g * (NW // 128)
            P.op("dve", lambda e: e.tensor_copy(mod_t[:, i, j0:j0 + NW // 128, :],
                                                tp_t[:, 0:2 * (NW // 128)].rearrange("p (j r) -> p j r", r=2)), reads=[tp_b], writes=[mod_b])
    tb = tabs["buf"]
    for i in range(DEPTH):
        for (an, bn, gn, g_t, base) in (("A1", "B1", "G1", gm_t, 0), ("A2", "B2", "G2", gf_t, 24)):
            for kc in range(KC):
                P.op("dve", lambda e, i=i, kc=kc, an=an, g_t=g_t, base=base: e.tensor_scalar(
                    tabs[an][:, i, kc, :], mod_t[:, i, base + 8 + kc, :], 1.0, g_t[:, i, kc:kc + 1], ALU.add, ALU.mult),
                    reads=[mod_b], writes=[tb])
            P.op("dve", lambda e, i=i, bn=bn, base=base: e.tensor_copy(tabs[bn][:, i, :, :], mod_t[:, i, base:base + 8, :]),
                 reads=[mod_b], writes=[tb])
            P.op("dve", lambda e, i=i, gn=gn, base=base: e.tensor_copy(tabs[gn][:, i, :, :], mod_t[:, i, base + 16:base + 24, :]),
                 reads=[mod_b], writes=[tb])


def ffn_blocks(last):
    blocks = [[(0, 1024, 0, 2, 0)], [(1024, 2048, 2, 2, 0)], [(2048, 3072, 2, 2, 0)]]
    if last:
        blocks.append([(3072, 4096, 2, 0, 0)])
    else:
        blocks.append([(3072, 4096, 2, 0, 0), (4096, 4352, 0, 0, 1)])
    return blocks


def stage_ffn(nc, S, li, last, hmid, hout, w_up, w_down, convT, tabs):
    P = S.P
    S.psum_pool(8)
    NB = 1024 + 4
    nrm = NormCtx(S, w=256)
    a_t, a_b = S.sb("a2", [128, KC, NB + 256], BF16)
    u_t, u_b = S.sb("u", [128, FC, 1280], BF16)
    wd_t, wd_b = S.sb("wd", [128, FC, D], BF16)
    cv_t, cv_b = S.sb("cv", [128, 4, FC], F32)
    gfull = RR(S.sbs("g", [128, NB], F32, 2))
    cbuf = RR(S.sbs("c", [128, 512], F32, 2))
    sbuf_ = RR(S.sbs("s", [128, 512], BF16, 2))
    wu = RR(S.sbs("wu", [128, KC, 256], BF16, 3))
    obuf = RR(S.sbs("o", [128, 512], F32, 3))
    hres = RR(S.sbs("hr", [128, 512], F32, 3))
    P.dma("sp", cv_t[:], convT[li], writes=[cv_b])
    P.dma("pool", wd_t[:], w_down[li].rearrange("(fc p) d -> p fc d", p=128), writes=[wd_b])
    wuv = w_up[li].rearrange("(kc p) f -> p kc f", p=128)
    A, Bv, G = tabs["A2"], tabs["B2"], tabs["G2"]
    tb = tabs["buf"]
    hv, hbuf = hmid
    ov, obuf_d = hout
    blocks = ffn_blocks(last)

    def plan(blk):
        segs, ntiles, off = [], [], 0
        for (s_, e_, hl, hr, mi) in blk:
            lo, hi = s_ - hl, e_ + hr
            for (t0, n) in col_tiles(lo, hi, 256, 4):
                ntiles.append((t0, n, off + (t0 - lo), mi))
            segs.append((s_, e_, hl, hr, mi, off, lo, hi))
            off += hi - lo
        return segs, ntiles

    def norm_tile(t0, n, aoff, mi):
        nrm.run((hv, hbuf), t0, n, a_t, a_b, aoff, A[:, li], Bv[:, li], mi)

    plans = [plan(blk) for blk in blocks]
    for nt in plans[0][1]:
        norm_tile(*nt)
    gctx = RR(S.sbs("gc", [128, TCX + 4], F32, 2))
    for (g_t, g_b) in gctx.items:
        P.op("pool", lambda e: e.memset(g_t[:, 0:2], 0.0), writes=[g_b])
        P.op("pool", lambda e: e.memset(g_t[:, TCX + 2:TCX + 4], 0.0), writes=[g_b])
    for bi, (segs, _) in enumerate(plans):
        for (s, e_, hl, hr, mi, off, lo, hi) in segs:
            if mi == 1:
                continue
            for (g_t, g_b) in gfull.items:
                if not hl:
                    P.op("pool", lambda e: e.memset(g_t[:, 0:2], 0.0), writes=[g_b])
                if not hr:
                    P.op("pool", lambda e: e.memset(g_t[:, e_ - s + 2:e_ - s + 4], 0.0), writes=[g_b])
        for j in range(FC):
            w_t, w_b = wu.next()
            P.dma("pool", w_t[:, :, 0:128], wuv[:, :, j * 128:(j + 1) * 128], writes=[w_b])
            P.dma("pool", w_t[:, :, 128:256], wuv[:, :, FF + j * 128:FF + (j + 1) * 128], writes=[w_b])
            uoff = 0
            for (s, e_, hl, hr, mi, off, lo, hi) in segs:
                g_t, g_b = (gctx if mi == 1 else gfull).next()
                nseg = e_ - s
                for (t0, n) in col_tiles(lo, hi):
                    ps_t, ps_b = S.ps()
                    for kc in range(KC):
                        P.op("pe", lambda e, kc=kc, w_t=w_t, ps_t=ps_t, c0=off + t0 - lo, n=n: e.matmul(
                            ps_t[:, 0:n], w_t[:, kc, 0:128], a_t[:, kc, c0:c0 + n], start=(kc == 0), stop=(kc == KC - 1)),
                            reads=[w_b, a_b], writes=[ps_b])
                    gc = t0 - s + 2
                    P.op("act", lambda e, g_t=g_t, ps_t=ps_t, gc=gc, n=n: e.copy(g_t[:, gc:gc + n], ps_t[:, 0:n]),
                         reads=[ps_b], writes=[g_b])
                for (t0, n) in col_tiles(s, e_):
                    ps_t, ps_b = S.ps()
                    for kc in range(KC):
                        P.op("pe", lambda e, kc=kc, w_t=w_t, ps_t=ps_t, c0=off + t0 - lo, n=n: e.matmul(
                            ps_t[:, 0:n], w_t[:, kc, 128:256], a_t[:, kc, c0:c0 + n], start=(kc == 0), stop=(kc == KC - 1)),
                            reads=[w_b, a_b], writes=[ps_b])
                    gc = t0 - s + 2
                    c_t, c_b = cbuf.next()
                    s_t, s_b = sbuf_.next()
                    P.op("act", lambda e, c_t=c_t, g_t=g_t, gc=gc, n=n, j=j: e.activation(
                        c_t[:, 0:n], g_t[:, gc:gc + n], AF.Identity, bias=cv_t[:, 3, j:j + 1], scale=cv_t[:, 1, j:j + 1]),
                        reads=[g_b, cv_b], writes=[c_b])
                    P.op("dve", lambda e, c_t=c_t, g_t=g_t, gc=gc, n=n, j=j: e.scalar_tensor_tensor(
                        c_t[:, 0:n], g_t[:, gc - 1:gc - 1 + n], cv_t[:, 0, j:j + 1], c_t[:, 0:n], ALU.mult, ALU.add),
                        reads=[g_b, cv_b, c_b], writes=[c_b])
                    P.op("dve", lambda e, c_t=c_t, g_t=g_t, gc=gc, n=n, j=j: e.scalar_tensor_tensor(
                        c_t[:, 0:n], g_t[:, gc + 1:gc + 1 + n], cv_t[:, 2, j:j + 1], c_t[:, 0:n], ALU.mult, ALU.add),
                        reads=[g_b, cv_b, c_b], writes=[c_b])
                    P.op("act", lambda e, c_t=c_t, s_t=s_t, n=n: e.activation(s_t[:, 0:n], c_t[:, 0:n], AF.Silu),
                         reads=[c_b], writes=[s_b])
                    uc = uoff + t0 - s
                    P.op("dve", lambda e, s_t=s_t, ps_t=ps_t, uc=uc, n=n, j=j: e.tensor_tensor(
                        u_t[:, j, uc:uc + n], s_t[:, 0:n], ps_t[:, 0:n], ALU.mult), reads=[s_b, ps_b], writes=[u_b])
                uoff += nseg
        pending = list(plans[bi + 1][1]) if bi + 1 < len(plans) else []
        groups = []
        uoff = 0
        for (s, e_, hl, hr, mi, off, lo, hi) in segs:
            for (t0, n) in col_tiles(s, e_):
                for dc in range(KC):
                    groups.append((t0, n, dc, mi, uoff + t0 - s))
            uoff += e_ - s
        every = max(1, len(groups) // (len(pending) + 2)) if pending else 0
        started = []
        for gi, (t0, n, dc, mi, uc) in enumerate(groups):
            h_t, h_b = hres.next()
            P.dma("sp", h_t[:, 0:n], hv[:, dc, t0:t0 + n], reads=[hbuf], writes=[h_b])
            o_t, o_b = obuf.next()
            ps_t, ps_b = S.ps()
            for fc in range(FC):
                P.op("pe", lambda e, fc=fc: e.matmul(
                    ps_t[:, 0:n], wd_t[:, fc, dc * 128:(dc + 1) * 128], u_t[:, fc, uc:uc + n],
                    start=(fc == 0), stop=(fc == FC - 1)), reads=[wd_b, u_b], writes=[ps_b])
            P.op("dve", lambda e: e.scalar_tensor_tensor(
                o_t[:, 0:n], ps_t[:, 0:n], G[:, li, dc, mi:mi + 1], h_t[:, 0:n], ALU.mult, ALU.add),
                reads=[ps_b, h_b], writes=[o_b])
            P.dma("act", ov[:, dc, t0:t0 + n], o_t[:, 0:n], reads=[o_b], writes=[Buf("od")])
            if every and (gi + 1) % every == 0:
                if started:
                    (nt, st_) = started.pop(0)
                    nrm.run_b(st_, a_t, a_b, nt[2], A[:, li], Bv[:, li], nt[3])
                if pending:
                    nt = pending.pop(0)
                    started.append((nt, nrm.run_a((hv, hbuf), nt[0], nt[1])))
        for (nt, st_) in started:
            nrm.run_b(st_, a_t, a_b, nt[2], A[:, li], Bv[:, li], nt[3])
        for nt in pending:
            norm_tile(*nt)


def stage_final(nc, S, hin, outT, gfinT):
    P = S.P
    S.psum_pool(4)
    nrm = NormCtx(S, w=512)
    gf_t, gf_b = S.sb("gfin", [128, KC, 1], F32)
    P.dma("sp", gf_t[:], gfinT, writes=[gf_b])
    ov = outT.rearrange("(kc p) t -> p kc t", p=128)
    ob = Buf("out")
    for (t0, n) in col_tiles(0, TL, 512):
        h_t, h_b, tmp_t, tmp_b = nrm.run(hin, t0, n, None, None, 0, gf_t, None, 0, tabb=[gf_b])
        P.dma("act", ov[:, :, t0:t0 + n], tmp_t[:, :, 0:n], reads=tmp_b, writes=[Buf("od")])


def stage_copy_mixer(nc, S, hin, hmid, last):
    hv, hb = hin
    mv, mb = hmid
    for (t0, n) in col_tiles(0, TL if last else T, 1088):
        S.P.dma("sp", mv[:, :, t0:t0 + n], hv[:, :, t0:t0 + n], writes=[Buf("x")])


MIXERS = {}
EXTRA_INPUTS = [
    ("w_sg_in", [D, 2 * D], F32), ("gsgT", [128, KC], F32), ("wsT", [128, 4, 128], F32), ("bsR", [128, 4, 128], F32),
    ("w_sg_out", [D, D], F32),
]
EXTRA_INPUTS += [
    ("w_fnet_out", [D, D], F32), ("cs256", [256, 512], BF16), ("csn256", [256, 512], BF16),
    ("dftC", [TL, TL], BF16), ("dftSn", [TL, TL], BF16),
]
EXTRA_INPUTS += [
    ("w_na_qkv", [D, 3 * D], F32), ("w_na_out", [D, D], F32), ("nab", [16, 64, 15 * 64], F32),
    ("w_att_qkv", [D, 1536], F32), ("w_att_out", [D, D], F32), ("gqk", [128, 2], F32),
    ("ident64", [64, 64], BF16), ("rotT", [128, 128], BF16), ("bd64", [128, 128], BF16), ("cosF", [128, T], F32), ("sinF", [128, T], F32),
]
SCRATCH = [("Zd", [T, 4, 512], BF16), ("YT", [D, T], BF16),
           ("QH", [16, 64, T], BF16), ("KH", [16, 64, T], BF16), ("VT", [T, D], BF16), ("OH", [16, 64, T], BF16)]


def emit_outproj_residual(S, li, src_t, src_b, w_t, w_b, h_t, h_b, n, t0, mi, G, ov, obufs):
    P = S.P
    for dc in range(KC):
        ps_t, ps_b = S.ps()
        for c in range(KC):
            P.op("pe", lambda e, c=c, dc=dc, ps_t=ps_t: e.matmul(ps_t[:, 0:n], w_t[:, c, dc * 128:(dc + 1) * 128], src_t[:, c, 0:n],
                                                               start=(c == 0), stop=(c == KC - 1)), reads=[w_b, src_b], writes=[ps_b])
        o_t, o_b = obufs.next()
        P.op("dve", lambda e, dc=dc, ps_t=ps_t, o_t=o_t: e.scalar_tensor_tensor(
            o_t[:, 0:n], ps_t[:, 0:n], G[:, li, dc, mi:mi + 1], h_t[:, dc, 0:n], ALU.mult, ALU.add), reads=[ps_b, h_b], writes=[o_b])
        P.dma("act", ov[:, dc, t0:t0 + n], o_t[:, 0:n], reads=[o_b], writes=[Buf("od")])


def mixer_sg(nc, li, last, hin, hmid, I, scratch, tabs):
    with Stage(nc, f"sg{li}") as S:
        P = S.P
        S.psum_pool(8)
        nrm = NormCtx(S, w=512, nh=2, nb=1)
        wu_t, wu_b = S.sb("wu", [128, KC, D], BF16)
        wv_t, wv_b = S.sb("wv", [128, KC, D], BF16)
        wo_t, wo_b = S.sb("wo", [128, KC, D], BF16)
        ws_t, ws_b = S.sb("ws", [128, 4, 128], BF16)
        bs_t, bs_b = S.sb("bs", [128, 4, 128], F32)
        gv_t, gv_b = S.sb("gv", [128, KC], F32)
        wiv = I["w_sg_in"].rearrange("(kc p) f -> p kc f", p=128)
        P.dma("pool", wu_t[:], wiv[:, :, 0:D], writes=[wu_b])
        P.dma("pool", wv_t[:], wiv[:, :, D:2 * D], writes=[wv_b])
        P.dma("pool", wo_t[:], I["w_sg_out"].rearrange("(kc p) f -> p kc f", p=128), writes=[wo_b])
        P.dma("pool", ws_t[:], I["wsT"], writes=[ws_b])
        P.dma("sp", bs_t[:], I["bsR"], writes=[bs_b])
        P.dma("sp", gv_t[:], I["gsgT"], writes=[gv_b])
        abufs = RR(S.sbs("a", [128, KC, 512], BF16, 2))
        u_t, u_b = S.sb("u", [128, KC, 512], F32)
        mx_t, mx_b = S.sb("mx", [128, KC, 512], F32)
        gt_t, gt_b = S.sb("gt", [128, KC, 512], BF16)
        vgs = RR(S.sbs("vg", [128, D], F32, 4))
        vss = RR(S.sbs("vs", [128, D], BF16, 4))
        sss = RR(S.sbs("ss", [128, 4], F32, 4))
        junk_t, junk_b = S.sb("junk", [128, D], BF16)
        obufs = RR(S.sbs("o", [128, 512], F32, 3))
        A, Bv, G = tabs["A1"], tabs["B1"], tabs["G1"]
        hv, hbuf = hin
        ov, _ = hmid
        tiles = [(t0, n, 0) for (t0, n) in col_tiles(0, TL, 512)] + ([] if last else [(TL, TCX, 1)])
        for (t0, n, mi) in tiles:
            a_t, a_b = abufs.next()
            h_t, h_b, _, _ = nrm.run((hv, hbuf), t0, n, a_t, a_b, 0, A[:, li], Bv[:, li], mi)
            for c in range(KC):
                ps_t, ps_b = S.ps()
                for kc in range(KC):
                    P.op("pe", lambda e, c=c, kc=kc, ps_t=ps_t, a_t=a_t: e.matmul(
                        ps_t[:, 0:n], wu_t[:, kc, c * 128:(c + 1) * 128], a_t[:, kc, 0:n], start=(kc == 0), stop=(kc == KC - 1)),
                        reads=[wu_b, a_b], writes=[ps_b])
                P.op("act", lambda e, c=c, ps_t=ps_t: e.activation(u_t[:, c, 0:n], ps_t[:, 0:n], AF.Gelu_apprx_tanh),
                     reads=[ps_b], writes=[u_b])
            def sg_a(q):
                vg_t, vg_b = vgs.next()
                vs_t, vs_b = vss.next()
                ss_t, ss_b = sss.next()
                P.op("pool", lambda e: e.memset(ss_t[:, 0:2], 0.0), writes=[ss_b])
                for half in range(2):
                    ps_t, ps_b = S.ps()
                    for kc in range(KC):
                        P.op("pe", lambda e, kc=kc: e.matmul(
                            ps_t[:, :], a_t[:, kc, q * 128:(q + 1) * 128], wv_t[:, kc, half * 512:(half + 1) * 512],
                            start=(kc == 0), stop=(kc == KC - 1)), reads=[wv_b, a_b], writes=[ps_b])
                    P.op("act", lambda e: e.activation(
                        vg_t[:, half * 512:(half + 1) * 512], ps_t[:, :], AF.Gelu_apprx_tanh), reads=[ps_b], writes=[vg_b])
                P.op("act", lambda e: e.activation(
                    junk_t[:, :], vg_t[:, :], AF.Square, accum_out=ss_t[:, 0:1]), reads=[vg_b, ss_b], writes=[ss_b, junk_b])
                P.op("act", lambda e: e.activation(ss_t[:, 2:4], ss_t[:, 0:2], AF.Sqrt, bias=EPS, scale=1.0 / D), reads=[ss_b], writes=[ss_b])
                P.op("dve", lambda e: e.reciprocal(ss_t[:, 2:4], ss_t[:, 2:4]), reads=[ss_b], writes=[ss_b])
                P.op("dve", lambda e: e.tensor_scalar(vs_t[:, :], vg_t[:, :], ss_t[:, 2:3], None, ALU.mult),
                     reads=[ss_b, vg_b], writes=[vs_b])
                return vs_t, vs_b

            def sg_b(q, vs_t, vs_b):
                for c4 in range(2):
                    ps_t, ps_b = S.ps()
                    for cc in range(4):
                        c = c4 * 4 + cc
                        P.op("pe", lambda e: e.matmul(
                            ps_t[:, cc * 128:(cc + 1) * 128], vs_t[:, c * 128:(c + 1) * 128], ws_t[:, c // 2, :], start=True, stop=True),
                            reads=[vs_b, ws_b], writes=[ps_b])
                    for cc in range(4):
                        c = c4 * 4 + cc
                        P.op("dve", lambda e: e.scalar_tensor_tensor(
                            mx_t[:, c, q * 128:(q + 1) * 128], ps_t[:, cc * 128:(cc + 1) * 128], gv_t[:, c:c + 1], bs_t[:, c // 2, :],
                            ALU.mult, ALU.add), reads=[ps_b, gv_b, bs_b], writes=[mx_b])

            nq_ = n // 128
            prev = sg_a(0)
            for q in range(1, nq_):
                cur = sg_a(q)
                sg_b(q - 1, *prev)
                prev = cur
            sg_b(nq_ - 1, *prev)
            for c in range(KC):
                P.op("dve", lambda e, c=c: e.tensor_tensor(gt_t[:, c, 0:n], u_t[:, c, 0:n], mx_t[:, c, 0:n], ALU.mult),
                     reads=[u_b, mx_b], writes=[gt_b])
            emit_outproj_residual(S, li, gt_t, gt_b, wo_t, wo_b, h_t, h_b, n, t0, mi, G, ov, obufs)


MIXERS[2] = mixer_sg


def mixer_fnet(nc, li, last, hin, hmid, I, scratch, tabs):
    Zd, YT = scratch["Zd"], scratch["YT"]
    A, Bv, G = tabs["A1"], tabs["B1"], tabs["G1"]
    hv, hbuf = hin
    ov, _ = hmid
    tiles = [(t0, n, 0) for (t0, n) in col_tiles(0, TL, 256)] + ([] if last else [(TL, TCX, 1)])
    with Stage(nc, f"fn1_{li}") as S:
        P = S.P
        S.psum_pool(8)
        nrm = NormCtx(S, w=256, nh=2, nb=2)
        cs_t, cs_b = S.sb("cs", [128, 2, 512], BF16)
        P.dma("sp", cs_t[:], I["cs256"].rearrange("(k p) c -> p k c", p=128), writes=[cs_b])
        abufs = RR(S.sbs("a", [128, KC, 256], BF16, 2))
        zbufs = RR(S.sbs("z", [128, 4, 512], BF16, 3))
        def norm_of(ti):
            t0_, n_, mi_ = tiles[ti]
            ab_ = abufs.next()
            nrm.run((hv, hbuf), t0_, n_, ab_[0], ab_[1], 0, A[:, li], Bv[:, li], mi_)
            return ab_

        a_next = norm_of(0)
        for ti, (t0, n, mi) in enumerate(tiles):
            a_t, a_b = a_next
            for q in range(n // 128):
                if q == 1 and ti + 1 < len(tiles):
                    a_next = norm_of(ti + 1)
                z_t, z_b = zbufs.next()
                for g in range(4):
                    ps_t, ps_b = S.ps()
                    for kk in range(2):
                        P.op("pe", lambda e, g=g, kk=kk, q=q, ps_t=ps_t, a_t=a_t: e.matmul(
                            ps_t[:, :], a_t[:, 2 * g + kk, q * 128:(q + 1) * 128], cs_t[:, kk, :], start=(kk == 0), stop=(kk == 1)),
                            reads=[a_b, cs_b], writes=[ps_b])
                    P.op("act" if g % 2 == 0 else "dve",
                         (lambda e, g=g, ps_t=ps_t, z_t=z_t: e.copy(z_t[:, g, :], ps_t[:, :])) if g % 2 == 0 else
                         (lambda e, g=g, ps_t=ps_t, z_t=z_t: e.tensor_copy(z_t[:, g, :], ps_t[:, :])),
                         reads=[ps_b], writes=[z_b])
                r0 = t0 + q * 128
                P.dma("sp", Zd[r0:r0 + 128, :, :], z_t[:], reads=[z_b], writes=[Buf("zd")])
    with Stage(nc, f"fn2_{li}") as S:
        P = S.P
        S.psum_pool(8)
        z_t, z_b = S.sb("z", [128, 32, 2, 512], BF16)
        cbufs = RR(S.sbs("c", [128, 8, 512], BF16, 3))
        sbufs = RR(S.sbs("s", [128, 8, 512], BF16, 3))
        ybufs = RR(S.sbs("y", [128, 512], BF16, 4))
        seqs = [(0, TL, I["dftC"], I["dftSn"], 512)]
        if not last:
            seqs.append((TL, TCX, I["csn256"][:, 0:256], I["csn256"][:, 256:512], 256))
        for (tok0, L, Cm, Sm, kbw) in seqs:
            nch = L // 128
            Cv = Cm.rearrange("(nc p) k -> p nc k", p=128)
            Sv = Sm.rearrange("(nc p) k -> p nc k", p=128)
            Zv = Zd[tok0:tok0 + L, :, :].rearrange("(nc p) g c -> p nc g c", p=128)
            for gp in range(2):
                P.dma("sp", z_t[:, 0:nch, :, :], Zv[:, :, 2 * gp:2 * gp + 2, :], writes=[z_b])
                for kb in range(L // kbw):
                    accs = [S.ps() for _ in range(4)]
                    npc = min(8, nch)
                    for pc in range(nch // npc):
                        c_t, c_b = cbufs.next()
                        s_t, s_b = sbufs.next()
                        P.dma("sp", c_t[:, 0:npc, 0:kbw], Cv[:, pc * npc:(pc + 1) * npc, kb * kbw:(kb + 1) * kbw], writes=[c_b])
                        P.dma("act", s_t[:, 0:npc, 0:kbw], Sv[:, pc * npc:(pc + 1) * npc, kb * kbw:(kb + 1) * kbw], writes=[s_b])
                        for nn in range(npc):
                            ni = pc * npc + nn
                            for ai, (ps_t, ps_b) in enumerate(accs):
                                gg, cc = ai // 2, ai % 2
                                P.op("pe", lambda e, ps_t=ps_t, ni=ni, nn=nn, gg=gg, cc=cc, c_t=c_t: e.matmul(
                                    ps_t[:, 0:kbw], z_t[:, ni, gg, cc * 128:(cc + 1) * 128], c_t[:, nn, 0:kbw],
                                    start=(ni == 0), stop=False), reads=[z_b, c_b], writes=[ps_b])
                                P.op("pe", lambda e, ps_t=ps_t, ni=ni, nn=nn, gg=gg, cc=cc, s_t=s_t: e.matmul(
                                    ps_t[:, 0:kbw], z_t[:, ni, gg, 256 + cc * 128:256 + (cc + 1) * 128], s_t[:, nn, 0:kbw],
                                    start=False, stop=(ni == nch - 1)), reads=[z_b, s_b], writes=[ps_b])
                    for ai, (ps_t, ps_b) in enumerate(accs):
                        gg, cc = ai // 2, ai % 2
                        chunk = (2 * gp + gg) * 2 + cc
                        y_t, y_b = ybufs.next()
                        if ai % 2 == 0:
                            P.op("act", lambda e, ps_t=ps_t, y_t=y_t: e.copy(y_t[:, 0:kbw], ps_t[:, 0:kbw]), reads=[ps_b], writes=[y_b])
                        else:
                            P.op("dve", lambda e, ps_t=ps_t, y_t=y_t: e.tensor_copy(y_t[:, 0:kbw], ps_t[:, 0:kbw]), reads=[ps_b], writes=[y_b])
                        c0 = tok0 + kb * kbw
                        P.dma("pool", YT[chunk * 128:(chunk + 1) * 128, c0:c0 + kbw], y_t[:, 0:kbw], reads=[y_b], writes=[Buf("yt")])
    with Stage(nc, f"fn3_{li}") as S:
        P = S.P
        S.psum_pool(8)
        wo_t, wo_b = S.sb("wo", [128, KC, D], BF16)
        P.dma("pool", wo_t[:], I["w_fnet_out"].rearrange("(kc p) f -> p kc f", p=128), writes=[wo_b])
        hb_ = RR(S.sbs("h", [128, KC, 256], F32, 2))
        yb_ = RR(S.sbs("y", [128, KC, 256], BF16, 2))
        obufs = RR(S.sbs("o", [128, 256], F32, 4))
        YTv = YT.rearrange("(kc p) t -> p kc t", p=128)
        for (t0, n, mi) in tiles:
            h_t, h_b = hb_.next()
            y_t, y_b = yb_.next()
            P.dma("sp", h_t[:, :, 0:n], hv[:, :, t0:t0 + n], writes=[h_b])
            P.dma("sp", y_t[:, :, 0:n], YTv[:, :, t0:t0 + n], writes=[y_b])
            emit_outproj_residual(S, li, y_t, y_b, wo_t, wo_b, h_t, h_b, n, t0, mi, G, ov, obufs)


MIXERS[0] = mixer_fnet


def attn_proj(nc, li, last, hin, I, scratch, tabs, wname, nq, nk, nv, rope, q_ctx):
    QH, KH, VT = scratch["QH"], scratch["KH"], scratch["VT"]
    A, Bv = tabs["A1"], tabs["B1"]
    hv, hbuf = hin
    tiles = [(t0, n, 0) for (t0, n) in col_tiles(0, TL, 256)] + [(TL, TCX, 1)]
    with Stage(nc, f"ap{li}") as S:
        P = S.P
        S.psum_pool(8, (128, 256))
        nrm = NormCtx(S, w=256, nh=2, nb=2)
        ncol = (nq + nk) * 128 + nv
        w_t, w_b = S.sb("w", [128, KC, ncol], BF16)
        wv_ = I[wname].rearrange("(kc p) f -> p kc f", p=128)
        for c0 in range(0, ncol, 512):
            c1 = min(ncol, c0 + 512)
            P.dma("pool", w_t[:, :, c0:c1], wv_[:, :, c0:c1], writes=[w_b])
        abufs = RR(S.sbs("a", [128, KC, 256], BF16, 2))
        qo = RR(S.sbs("qo", [128, 256], BF16, 4))
        vo = RR(S.sbs("vo", [128, 256], BF16, 4))
        if rope:
            rot_t, rot_b = S.sb("rot", [128, 128], BF16)
            bd_t, bd_b = S.sb("bd", [128, 128], BF16)
            g_t, g_b = S.sb("g", [128, 2], F32)
            cos_t, cos_b = S.sb("cos", [128, T], F32)
            sin_t, sin_b = S.sb("sin", [128, T], F32)
            P.dma("sp", rot_t[:], I["rotT"], writes=[rot_b])
            P.dma("sp", bd_t[:], I["bd64"], writes=[bd_b])
            P.dma("sp", g_t[:], I["gqk"], writes=[g_b])
            P.dma("sp", cos_t[:], I["cosF"], writes=[cos_b])
            P.dma("act", sin_t[:], I["sinF"], writes=[sin_b])
            qf = RR(S.sbs("qf", [128, 256], F32, 6))
            sq = RR(S.sbs("sq", [128, 256], BF16, 4))
            rs = RR(S.sbs("rs", [128, 256], F32, 4))
            qn = RR(S.sbs("qn", [128, 256], BF16, 6))
            t1 = RR(S.sbs("t1", [128, 256], F32, 3))
            t2 = RR(S.sbs("t2", [128, 256], F32, 3))
        def proj_mm(ps_t, ps_b, a_t, a_b, c, n):
            for kc in range(KC):
                P.op("pe", lambda e, kc=kc: e.matmul(
                    ps_t[:, 0:n], w_t[:, kc, c * 128:(c + 1) * 128], a_t[:, kc, 0:n], start=(kc == 0), stop=(kc == KC - 1)),
                    reads=[w_b, a_b], writes=[ps_b])

        def store(o_t, o_b, c, t0, n):
            isq = c < nq
            dst = QH if isq else KH
            hc = c if isq else c - nq
            P.dma("sp", dst[2 * hc:2 * hc + 2, :, t0:t0 + n].rearrange("h d t -> (h d) t"), o_t[:, 0:n], reads=[o_b], writes=[Buf("qk")])

        def vproj(a_t, a_b, t0, n):
            for q in range(n // 128):
                for vc in range(0, nv, 256):
                    ps_t, ps_b = S.ps()
                    for kc in range(KC):
                        P.op("pe", lambda e, kc=kc: e.matmul(
                            ps_t[:, :], a_t[:, kc, q * 128:(q + 1) * 128], w_t[:, kc, (nq + nk) * 128 + vc:(nq + nk) * 128 + vc + 256],
                            start=(kc == 0), stop=(kc == KC - 1)), reads=[w_b, a_b], writes=[ps_b])
                    v_t, v_b = vo.next()
                    P.op("act", lambda e: e.copy(v_t[:, :], ps_t[:, :]), reads=[ps_b], writes=[v_b])
                    r0 = t0 + q * 128
                    P.dma("act", VT[r0:r0 + 128, vc:vc + 256], v_t[:, :], reads=[v_b], writes=[Buf("v")])

        if not rope:
            def norm_of(ti):
                t0_, n_, mi_ = tiles[ti]
                ab_ = abufs.next()
                nrm.run((hv, hbuf), t0_, n_, ab_[0], ab_[1], 0, A[:, li], Bv[:, li], mi_)
                return ab_

            a_next = norm_of(0)
            for ti, (t0, n, mi) in enumerate(tiles):
                a_t, a_b = a_next
                for c in range(nq + nk):
                    if c == 6 and ti + 1 < len(tiles):
                        a_next = norm_of(ti + 1)
                    if c < nq and mi == 1 and not q_ctx:
                        continue
                    ps_t, ps_b = S.ps()
                    proj_mm(ps_t, ps_b, a_t, a_b, c, n)
                    o_t, o_b = qo.next()
                    if c % 2 == 0:
                        P.op("act", lambda e: e.copy(o_t[:, 0:n], ps_t[:, 0:n]), reads=[ps_b], writes=[o_b])
                    else:
                        P.op("dve", lambda e: e.tensor_copy(o_t[:, 0:n], ps_t[:, 0:n]), reads=[ps_b], writes=[o_b])
                    store(o_t, o_b, c, t0, n)
                vproj(a_t, a_b, t0, n)
        else:
            units = []
            for ti, (t0, n, mi) in enumerate(tiles):
                k = 0
                for c in range(nq + nk):
                    if c < nq and mi == 1 and not q_ctx:
                        continue
                    units.append(dict(t0=t0, n=n, mi=mi, c=c, ti=ti, k=k))
                    k += 1
                units[-1]["lastc"] = True
            NU = len(units)
            tile_a = {}

            def norm_of(ti):
                if ti not in tile_a and ti < len(tiles):
                    t0_, n_, mi_ = tiles[ti]
                    ab_ = abufs.next()
                    nrm.run((hv, hbuf), t0_, n_, ab_[0], ab_[1], 0, A[:, li], Bv[:, li], mi_)
                    tile_a[ti] = ab_

            def st0(u):
                t0, n, c = u["t0"], u["n"], u["c"]
                norm_of(u["ti"])
                if u["k"] == 4:
                    norm_of(u["ti"] + 1)
                a_t, a_b = tile_a[u["ti"]]
                ps_t, ps_b = S.ps()
                proj_mm(ps_t, ps_b, a_t, a_b, c, n)
                u["qf"] = qf.next()
                u["sq"] = sq.next()
                qf_t, qf_b = u["qf"]
                sq_t, sq_b = u["sq"]
                P.op("act", lambda e: e.copy(qf_t[:, 0:n], ps_t[:, 0:n]), reads=[ps_b], writes=[qf_b])
                P.op("act", lambda e: e.activation(sq_t[:, 0:n], ps_t[:, 0:n], AF.Square), reads=[ps_b], writes=[sq_b])
                if u.get("lastc"):
                    vproj(a_t, a_b, t0, n)

            def st1(u):
                n, c = u["n"], u["c"]
                gi = 0 if c < nq else 1
                qf_t, qf_b = u["qf"]
                sq_t, sq_b = u["sq"]
                rs_t, rs_b = rs.next()
                u["qn"] = qn.next()
                qn_t, qn_b = u["qn"]
                p2_t, p2_b = S.ps()
                P.op("pe", lambda e: e.matmul(p2_t[:, 0:n], bd_t[:], sq_t[:, 0:n], start=True, stop=True), reads=[sq_b, bd_b], writes=[p2_b])
                P.op("act", lambda e: e.activation(rs_t[:, 0:n], p2_t[:, 0:n], AF.Sqrt, bias=EPS, scale=1.0 / 64), reads=[p2_b], writes=[rs_b])
                P.op("dve", lambda e: e.reciprocal(rs_t[:, 0:n], rs_t[:, 0:n]), reads=[rs_b], writes=[rs_b])
                P.op("dve", lambda e: e.scalar_tensor_tensor(qn_t[:, 0:n], qf_t[:, 0:n], g_t[:, gi:gi + 1], rs_t[:, 0:n], ALU.mult, ALU.mult),
                     reads=[qf_b, rs_b, g_b], writes=[qn_b])

            def st2(u):
                t0, n, c = u["t0"], u["n"], u["c"]
                qn_t, qn_b = u["qn"]
                t1_t, t1_b = t1.next()
                t2_t, t2_b = t2.next()
                o_t, o_b = qo.next()
                p3_t, p3_b = S.ps()
                P.op("pe", lambda e: e.matmul(p3_t[:, 0:n], rot_t[:], qn_t[:, 0:n], start=True, stop=True), reads=[qn_b, rot_b], writes=[p3_b])
                P.op("dve", lambda e: e.tensor_tensor(t1_t[:, 0:n], qn_t[:, 0:n], cos_t[:, t0:t0 + n], ALU.mult), reads=[qn_b, cos_b], writes=[t1_b])
                P.op("dve", lambda e: e.tensor_tensor(t2_t[:, 0:n], p3_t[:, 0:n], sin_t[:, t0:t0 + n], ALU.mult), reads=[p3_b, sin_b], writes=[t2_b])
                P.op("pool", lambda e: e.tensor_tensor(o_t[:, 0:n], t1_t[:, 0:n], t2_t[:, 0:n], ALU.add), reads=[t1_b, t2_b], writes=[o_b])
                store(o_t, o_b, c, t0, n)

            for i in range(NU + 4):
                if i < NU:
                    st0(units[i])
                if 0 <= i - 2 < NU:
                    st1(units[i - 2])
                if 0 <= i - 4 < NU:
                    st2(units[i - 4])


def attn_out(nc, li, last, hin, hmid, I, scratch, tabs, wname):
    OH = scratch["OH"]
    G = tabs["G1"]
    hv, hbuf = hin
    ov, _ = hmid
    tiles = [(t0, n, 0) for (t0, n) in col_tiles(0, TL, 256)] + ([] if last else [(TL, TCX, 1)])
    with Stage(nc, f"ao{li}") as S:
        P = S.P
        S.psum_pool(8, (128, 256))
        wo_t, wo_b = S.sb("wo", [128, KC, D], BF16)
        P.dma("pool", wo_t[:], I[wname].rearrange("(kc p) f -> p kc f", p=128), writes=[wo_b])
        hb_ = RR(S.sbs("h", [128, KC, 256], F32, 2))
        yb_ = RR(S.sbs("y", [128, KC, 256], BF16, 2))
        obufs = RR(S.sbs("o", [128, 256], F32, 4))
        OHv = OH.rearrange("(c two) d t -> (two d) c t", two=2)
        for (t0, n, mi) in tiles:
            h_t, h_b = hb_.next()
            y_t, y_b = yb_.next()
            P.dma("sp", h_t[:, :, 0:n], hv[:, :, t0:t0 + n], writes=[h_b])
            P.dma("sp", y_t[:, :, 0:n], OHv[:, :, t0:t0 + n], writes=[y_b])
            emit_outproj_residual(S, li, y_t, y_b, wo_t, wo_b, h_t, h_b, n, t0, mi, G, ov, obufs)


def attn_pipeline(S, items, spp, pbs, tmps, LA=3):
    P = S.P
    n = len(items)
    sbanks = [None] * n

    def issue_s(i):
        it = items[i]
        if it.get("pre") is not None:
            it["pre"]()
        s_t, s_b = spp.next()
        sbanks[i] = (s_t, s_b)
        mk, ncol = it["mk"], it["ncol"]
        (k_ap, k_b), (q_ap, q_b) = it["k"], it["q"]
        if it["bias"] is None:
            P.op("pe", lambda e: e.matmul(s_t[0:mk, 0:ncol], k_ap, q_ap, start=True, stop=True), reads=[k_b, q_b], writes=[s_b])
        else:
            (b_ap, b_b), (i_ap, i_b) = it["bias"], it["ident"]
            P.op("pe", lambda e: e.matmul(s_t[0:mk, 0:ncol], k_ap, q_ap, start=True, stop=False), reads=[k_b, q_b], writes=[s_b])
            P.op("pe", lambda e: e.matmul(s_t[0:mk, 0:ncol], i_ap, b_ap, start=False, stop=True), reads=[b_b, i_b], writes=[s_b])

    for i in range(min(LA, n)):
        issue_s(i)
    for i in range(n):
        if i + LA < n:
            issue_s(i + LA)
        it = items[i]
        s_t, s_b = sbanks[i]
        mk, ncol, c0 = it["mk"], it["ncol"], it["c0"]
        p_t, p_b = pbs.next()
        P.op("act", lambda e: e.activation(p_t[0:mk, 0:ncol], s_t[0:mk, 0:ncol], AF.Exp, scale=0.125), reads=[s_b], writes=[p_b])
        (v_ap, v_b), (acc_t, acc_b) = it["v"], it["acc"]
        first, lastw = it["first"], it["last"]
        kp = it.get("kp", mk)
        P.op("pe", lambda e: e.matmul(acc_t[:, c0:c0 + ncol], v_ap, p_t[0:kp, 0:ncol], start=first, stop=lastw, skip_group_check=True),
             reads=[v_b, p_b], writes=[acc_b])
        if it["fin"] is not None:
            it["fin"]()


def gqa_pipeline(S, items, spp, pbs, LA=2):
    P = S.P
    assert len(items) % 2 == 0
    ng = len(items) // 2
    sb = [None] * ng

    def issue_s(g):
        s_t, s_b = spp.next()
        sb[g] = (s_t, s_b)
        for j in range(2):
            it = items[2 * g + j]
            if it.get("pre") is not None:
                it["pre"]()
            (k_ap, k_b), (q_ap, q_b) = it["k"], it["q"]
            P.op("pe", lambda e: e.matmul(s_t[:, j * 512:(j + 1) * 512], k_ap, q_ap, start=True, stop=True), reads=[k_b, q_b], writes=[s_b])

    for g in range(min(LA, ng)):
        issue_s(g)
    for g in range(ng):
        if g + LA < ng:
            issue_s(g + LA)
        s_t, s_b = sb[g]
        p_t, p_b = pbs.next()
        P.op("act", lambda e: e.activation(p_t[:, :], s_t[:, :], AF.Exp, scale=0.125), reads=[s_b], writes=[p_b])
        for j in range(2):
            it = items[2 * g + j]
            (v_ap, v_b), (acc_t, acc_b) = it["v"], it["acc"]
            first, lastw = it["first"], it["last"]
            P.op("pe", lambda e: e.matmul(acc_t, v_ap, p_t[:, j * 512:(j + 1) * 512], start=first, stop=lastw, skip_group_check=True),
                 reads=[v_b, p_b], writes=[acc_b])
            if it["fin"] is not None:
                it["fin"]()


def attn_finish(S, acc, ncols, rec, obuf, dst_ap):
    P = S.P
    acc_t, acc_b = acc
    r_t, r_b = rec.next()
    ob_t, ob_b = obuf.next()
    P.op("dve", lambda e: e.reciprocal(r_t[0:64, 0:ncols], acc_t[64:128, 0:ncols]), reads=[acc_b], writes=[r_b])
    P.op("dve", lambda e: e.tensor_tensor(ob_t[0:64, 0:ncols], acc_t[0:64, 0:ncols], r_t[0:64, 0:ncols], ALU.mult),
         reads=[acc_b, r_b], writes=[ob_b])
    P.dma("sp", dst_ap, ob_t[0:64, 0:ncols], reads=[ob_b], writes=[Buf("oh")])


def mixer_gqa(nc, li, last, hin, hmid, I, scratch, tabs):
    QH, KH, VT, OH = scratch["QH"], scratch["KH"], scratch["VT"], scratch["OH"]
    attn_proj(nc, li, last, hin, I, scratch, tabs, "w_att_qkv", 8, 2, 256, True, False)
    with Stage(nc, f"ga{li}") as S:
        P = S.P
        S.psum_pool(4, (128, 1024))
        acc_t0 = S.psums[0][0]
        accp = RR([(acc_t0[:, 0:512], Buf("accA")), (acc_t0[:, 512:1024], Buf("accB"))])
        spp = RR(S.psums[1:4])
        NKT = T // 128
        kts = RR(S.sbs("kt", [128, T], BF16, 2))
        vts = RR(S.sbs("vt", [128, NKT, 128], BF16, 2))
        for (v_t, v_b) in vts.items:
            P.op("pool", lambda e: e.memset(v_t[:, :, 64:128], 1.0), writes=[v_b])
        qbs = RR(S.sbs("q", [128, 512], BF16, 3))
        for (z_t, z_b) in kts.items + qbs.items:
            P.op("pool", lambda e: e.memset(z_t[64:128, :], 0.0), writes=[z_b])
        pbs = RR(S.sbs("p", [128, 1024], BF16, 3))
        rec = RR(S.sbs("rec", [64, 512], F32, 2))
        obuf = RR(S.sbs("ob", [64, 512], BF16, 2))
        items = []
        tiles_q = []
        for kh in range(4):
            kt_t, kt_b = kts.next()
            vt_t, vt_b = vts.next()
            for qt in range(TL // 128):
                tiles_q.append((kh, qt, kt_t, kt_b, vt_t, vt_b, qbs.next(), accp.next()))

        def make_load(ti):
            kh, qt, kt_t, kt_b, vt_t, vt_b, (q_t, q_b), acc = tiles_q[ti]

            def load():
                if qt == 0:
                    P.dma("sp", kt_t[0:64, :], KH[kh], writes=[kt_b])
                    VTv = VT[:, kh * 64:(kh + 1) * 64].rearrange("(n p) d -> p n d", p=128)
                    for a in range(0, NKT, 17):
                        P.dma("act", vt_t[:, a:a + 17, 0:64], VTv[:, a:a + 17, :], writes=[vt_b])
                qsrc = QH[4 * kh:4 * kh + 4, :, qt * 128:(qt + 1) * 128].rearrange("h d t -> d h t")
                P.dma("sp", q_t[0:64, :].rearrange("d (h t) -> d h t", h=4), qsrc, writes=[q_b])
            return load

        for ti, (kh, qt, kt_t, kt_b, vt_t, vt_b, (q_t, q_b), acc) in enumerate(tiles_q):
            dst = OH[4 * kh:4 * kh + 4, :, qt * 128:(qt + 1) * 128].rearrange("h d t -> d h t")

            def fin(acc=acc, dst=dst):
                attn_finish(S, acc, 512, rec, obuf, dst)

            def pre(ti=ti):
                if ti == 0:
                    make_load(0)()
                if ti + 1 < len(tiles_q):
                    make_load(ti + 1)()

            for kt in range(NKT):
                items.append(dict(k=(kt_t[:, kt * 128:(kt + 1) * 128], kt_b), q=(q_t[:, :], q_b), mk=128, ncol=512, bias=None,
                                  v=(vt_t[:, kt, :], vt_b), acc=acc, c0=0, first=(kt == 0), last=(kt == NKT - 1),
                                  fin=fin if kt == NKT - 1 else None, pre=pre if kt == 0 else None))
        gqa_pipeline(S, items, spp, pbs, LA=2)
    attn_out(nc, li, last, hin, hmid, I, scratch, tabs, "w_att_out")


MIXERS[3] = mixer_gqa


def na_rowsets():
    rs = lambda r: min(max(r - 4, 0), 56)
    out = []
    for j in range(8):
        items = []
        for kr in range(64):
            qs = [qr for qr in range(8 * j, 8 * j + 8) if rs(qr) <= kr < rs(qr) + 8]
            if qs:
                assert qs == list(range(qs[0], qs[-1] + 1))
                items.append((kr, qs[0], qs[-1] + 1))
        out.append(items)
    return out


def mixer_na(nc, li, last, hin, hmid, I, scratch, tabs):
    QH, KH, VT, OH = scratch["QH"], scratch["KH"], scratch["VT"], scratch["OH"]
    attn_proj(nc, li, last, hin, I, scratch, tabs, "w_na_qkv", 8, 8, D, False, True)
    rowsets = na_rowsets()
    with Stage(nc, f"na{li}") as S:
        P = S.P
        S.psum_pool(8)
        accp = RR(S.psums[0:2])
        spp = RR(S.psums[2:8])
        NR = T // 64
        kts = RR(S.sbs("kt", [128, T], BF16, 2))
        qhs = RR(S.sbs("qh", [128, T], BF16, 2))
        vts = RR(S.sbs("vt", [128, NR, 128], BF16, 2))
        for (v_t, v_b) in vts.items:
            P.op("pool", lambda e: e.memset(v_t[0:64, :, 64:128], 1.0), writes=[v_b])
        tts = RR(S.sbs("tt", [64, 15 * 64], F32, 2))
        tt8s = RR(S.sbs("tt8", [128, 15 * 64], BF16, 2))
        id_t, id_b = S.sb("ident", [128, 64], BF16)
        P.dma("sp", id_t[0:64, :], I["ident64"], writes=[id_b])
        pbs = RR(S.sbs("p", [128, 512], BF16, 4))
        for (z_t, z_b) in kts.items + qhs.items + tt8s.items + pbs.items + [(id_t, id_b)]:
            P.op("pool", lambda e: e.memset(z_t[64:128, :], 0.0), writes=[z_b])
        for (v_t, v_b) in vts.items:
            P.op("pool", lambda e: e.memset(v_t[64:128, :, :], 0.0), writes=[v_b])
        rec = RR(S.sbs("rec", [64, 512], F32, 2))
        obuf = RR(S.sbs("ob", [64, 512], BF16, 2))
        items = []
        for h in range(16):
            kt_t, kt_b = kts.next()
            qh_t, qh_b = qhs.next()
            vt_t, vt_b = vts.next()
            tt_t, tt_b = tts.next()
            t8_t, t8_b = tt8s.next()
            VTv = VT[:, h * 64:(h + 1) * 64].rearrange("(r p) d -> p r d", p=64)

            def load(h=h, kt_t=kt_t, kt_b=kt_b, qh_t=qh_t, qh_b=qh_b, vt_t=vt_t, vt_b=vt_b, tt_t=tt_t, tt_b=tt_b, VTv=VTv,
                     t8_t=t8_t, t8_b=t8_b):
                P.dma("sp", kt_t[0:64, :], KH[h], writes=[kt_b])
                P.dma("sp", qh_t[0:64, :], QH[h], writes=[qh_b])
                P.dma("sp", tt_t[:], I["nab"][h], writes=[tt_b])
                P.op("dve", lambda e: e.tensor_scalar(t8_t[0:64, :], tt_t[:, :], 8.0, None, ALU.mult), reads=[tt_b], writes=[t8_b])
                for a in range(0, NR, 17):
                    P.dma("act", vt_t[0:64, a:a + 17, 0:64], VTv[:, a:a + 17, :], writes=[vt_b])

            bands = [(512 * j, 512, rowsets[j]) for j in range(8)] + ([] if last else [(TL, TCX, [])])
            for bi, (q0, nq_, rows) in enumerate(bands):
                acc = accp.next()
                work = [(64 + i, 0, nq_, None) for i in range(4)]
                for (kr, qa, qb) in rows:
                    work.append((kr, (qa * 64) - q0, (qb * 64) - q0, (qa - kr + 7) * 64))

                def fin(acc=acc, h=h, q0=q0, nq_=nq_):
                    attn_finish(S, acc, nq_, rec, obuf, OH[h, :, q0:q0 + nq_])

                for wi, (kr, c0, c1, boff) in enumerate(work):
                    ncol = c1 - c0
                    items.append(dict(k=(kt_t[:, kr * 64:(kr + 1) * 64], kt_b), q=(qh_t[:, q0 + c0:q0 + c1], qh_b), mk=64, ncol=ncol,
                                      bias=None if boff is None else (t8_t[:, boff:boff + ncol], t8_b), ident=(id_t[:, :], id_b),
                                      v=(vt_t[:, kr, :], vt_b), kp=128, acc=acc, c0=c0, first=(wi == 0), last=(wi == len(work) - 1),
                                      fin=fin if wi == len(work) - 1 else None, pre=load if (bi == 0 and wi == 0) else None))
        attn_pipeline(S, items, spp, pbs, None, LA=4)
    attn_out(nc, li, last, hin, hmid, I, scratch, tabs, "w_na_out")


MIXERS[1] = mixer_na
SEMS = [None]
DEBUG_OUT = [False]
SCOPES = [False]


def build(depth=DEPTH, mixers=(0, 1, 2, 3), do_final=True):
    nc = bass.Bass("TRN2", target_bir_lowering=False)
    dt = lambda name, shape, dtype=F32, kind="ExternalInput": nc.dram_tensor(name, list(shape), dtype, kind=kind).ap()
    I = {}
    I["xT"] = dt("xT", [D, T])
    I["condT"] = dt("condT", [128, KC, 2])
    I["w_mod"] = dt("w_mod", [DEPTH, D, 6 * D])
    I["bmod2"] = dt("bmod2", [2, DEPTH, 6 * D])
    I["ident2"] = dt("ident2", [2, 2])
    I["gmixT"] = dt("gmixT", [128, DEPTH, KC])
    I["gffnT"] = dt("gffnT", [128, DEPTH, KC])
    I["gfinT"] = dt("gfinT", [128, KC, 1])
    I["w_ffn_up"] = dt("w_ffn_up", [DEPTH, D, 2 * FF])
    I["w_ffn_down"] = dt("w_ffn_down", [DEPTH, FF, D])
    I["convT"] = dt("convT", [DEPTH, 128, 4, FC])
    for name, shape, dtype in EXTRA_INPUTS:
        I[name] = dt(name, shape, dtype)
    outT = dt("outT", [D, TL], F32, "ExternalOutput")
    hA = dt("hA", [D, T], F32, "Internal")
    hB = dt("hB", [D, T], F32, "Internal")
    scratch = {name: dt(name, shape, dtype, "ExternalOutput" if DEBUG_OUT[0] else "Internal") for name, shape, dtype in SCRATCH}
    with ExitStack() as st:
        SEMS[0] = Sems(nc, st)
        tabs = {k: st.enter_context(nc.sbuf_tensor(f"tab_{k}", [128, DEPTH, KC, 2], F32))
                for k in ("A1", "B1", "G1", "A2", "B2", "G2")}
        tabs["buf"] = Buf("tabs")
        with Stage(nc, "ada") as S:
            stage_ada(nc, S, I["condT"], I["w_mod"], I["bmod2"], I["gmixT"], I["gffnT"], I["ident2"], tabs)
        hin = (hview(I["xT"]), Buf("xT"))
        vA, vB = hview(hA), hview(hB)
        for li in range(depth):
            last = li == DEPTH - 1
            hmid = (vB, Buf("hB"))
            hout = (vA, Buf("hA"))
            mx = MIXERS.get(mixers[li]) if mixers[li] is not None else None
            if mx is None:
                with Stage(nc, f"mx{li}") as S:
                    stage_copy_mixer(nc, S, hin, hmid, last)
            else:
                mx(nc, li, last, hin, hmid, I, scratch, tabs)
            with Stage(nc, f"ffn{li}") as S:
                stage_ffn(nc, S, li, last, hmid, hout, I["w_ffn_up"], I["w_ffn_down"], I["convT"], tabs)
            hin = (vA, Buf("hA"))
        if do_final:
            with Stage(nc, "final") as S:
                stage_final(nc, S, hin, outT, I["gfinT"])
        else:
            with Stage(nc, "dump") as S:
                ov = outT.rearrange("(kc p) t -> p kc t", p=128)
                for (t0, n) in col_tiles(0, TL, 1024):
                    S.P.dma("sp", ov[:, :, t0:t0 + n], hin[0][:, :, t0:t0 + n], writes=[Buf("x")])
    return nc


def cols(v):
    v = np.asarray(v, np.float32)
    lead = v.shape[:-1]
    n = v.shape[-1] // 128
    r = v.reshape(*lead, n, 128)
    return np.ascontiguousarray(np.moveaxis(r, -1, 0))


_CONSTS = {}


def dft_consts():
    if not _CONSTS:
        bf = ml_dtypes.bfloat16
        n = np.arange(256, dtype=np.int64)
        ang = 2.0 * np.pi * ((n[:, None] * n[None, :]) % 256).astype(np.float64) / 256.0
        c, s_ = np.cos(ang) / 16.0, np.sin(ang) / 16.0
        _CONSTS["cs256"] = np.ascontiguousarray(np.concatenate([c, s_], 1).astype(np.float32).astype(bf))
        _CONSTS["csn256"] = np.ascontiguousarray(np.concatenate([c, -s_], 1).astype(np.float32).astype(bf))
        n = np.arange(TL, dtype=np.int64)
        ang = 2.0 * np.pi * ((n[:, None] * n[None, :]) % TL).astype(np.float64) / TL
        _CONSTS["dftC"] = np.ascontiguousarray((np.cos(ang) / 64.0).astype(np.float32).astype(bf))
        _CONSTS["dftSn"] = np.ascontiguousarray((-np.sin(ang) / 64.0).astype(np.float32).astype(bf))
    return _CONSTS


def rope_consts():
    if "rotT" not in _CONSTS:
        bf = ml_dtypes.bfloat16
        rot = np.zeros((128, 128), np.float32)
        for hh in range(2):
            for m_ in range(32):
                rot[hh * 64 + m_ + 32, hh * 64 + m_] = -1.0
                rot[hh * 64 + m_, hh * 64 + m_ + 32] = 1.0
        _CONSTS["rotT"] = rot.astype(bf)
        _CONSTS["ident64"] = np.eye(64, dtype=np.float32).astype(bf)
        bd = np.zeros((128, 128), np.float32)
        bd[0:64, 0:64] = 1.0
        bd[64:128, 64:128] = 1.0
        _CONSTS["bd64"] = bd.astype(bf)
        t = np.arange(TL)
        row = (t // 64).astype(np.float32)
        col = (t % 64).astype(np.float32)
        inv = (np.float32(10000.0) ** (-np.arange(16, dtype=np.float32) / np.float32(16))).astype(np.float32)
        ang = np.concatenate([row[:, None] * inv, col[:, None] * inv], -1).astype(np.float32)
        cos = np.cos(ang).astype(np.float32).T
        sin = np.sin(ang).astype(np.float32).T
        cosF = np.ones((128, T), np.float32)
        sinF = np.zeros((128, T), np.float32)
        for r in range(4):
            cosF[r * 32:(r + 1) * 32, :TL] = cos
            sinF[r * 32:(r + 1) * 32, :TL] = sin
        _CONSTS["cosF"], _CONSTS["sinF"] = cosF, sinF
    return {k: _CONSTS[k] for k in ("rotT", "ident64", "bd64", "cosF", "sinF")}


def na_bias_table(rpb):
    kc = np.arange(64)[:, None]
    qc = np.arange(64)[None, :]
    cs = np.clip(qc - 8, 0, 48)
    valid = (kc >= cs) & (kc < cs + 16)
    coff = np.clip(kc - qc + 15, 0, 30)
    out = np.full((16, 64, 15, 64), -30000.0, np.float32)
    for e in range(15):
        g = rpb[:, 14 - e][:, coff]
        out[:, :, e, :] = np.where(valid[None], g, np.float32(-30000.0))
    return np.ascontiguousarray(out.reshape(16, 64, 15 * 64))


def prep_inputs(inp, b):
    m = {}
    m["xT"] = np.ascontiguousarray(np.concatenate([inp["x"][b], inp["ctx"][b]], axis=0).T)
    m["condT"] = np.ascontiguousarray(cols(np.stack([inp["c"][b], inp["c_ctx"]], 0)).transpose(0, 2, 1))
    m["w_mod"] = inp["w_mod"]
    m["bmod2"] = np.ascontiguousarray(np.broadcast_to(inp["b_mod"][None].astype(np.float32), (2, DEPTH, 6 * D)))
    m["ident2"] = np.eye(2, dtype=np.float32)
    m["gmixT"] = cols(inp["g_norm_mix"])
    m["gffnT"] = cols(inp["g_norm_ffn"])
    m["gfinT"] = cols(inp["g_final"])[:, :, None].copy()
    m["w_ffn_up"] = inp["w_ffn_up"]
    m["w_ffn_down"] = inp["w_ffn_down"]
    cv = np.concatenate([inp["w_ffn_conv"], inp["b_ffn_conv"][:, None, :]], axis=1)
    m["convT"] = np.ascontiguousarray(cols(cv).transpose(1, 0, 2, 3))
    m.update(dft_consts())
    m["w_fnet_out"] = inp["w_fnet_out"][0]
    m.update(rope_consts())
    m["w_na_qkv"] = inp["w_na_qkv"][0]
    m["w_na_out"] = inp["w_na_out"][0]
    m["nab"] = na_bias_table(inp["na_rel_bias"][0])
    m["w_att_qkv"] = inp["w_att_qkv"][0]
    m["w_att_out"] = inp["w_att_out"][0]
    m["gqk"] = np.ascontiguousarray(np.stack([np.tile(inp["g_att_q"][0], 2), np.tile(inp["g_att_k"][0], 2)], 1).astype(np.float32))
    m["w_sg_in"] = inp["w_sg_in"][0]
    m["w_sg_out"] = inp["w_sg_out"][0]
    m["gsgT"] = cols(inp["g_sg_v"][0])
    m["wsT"] = np.ascontiguousarray(inp["w_sg_spatial"][0].transpose(2, 0, 1))
    m["bsR"] = np.ascontiguousarray(np.broadcast_to(inp["b_sg_spatial"][0][None], (128, 4, 128)))
    return m


REAL_CORES = (0, 1, 4, 5)


def kernel(**inp):
    inp = {k: np.asarray(v) for k, v in inp.items()}
    nc = build()
    in_maps = [None] * 8
    for b, core in enumerate(REAL_CORES):
        in_maps[core] = prep_inputs(inp, b)
    zeros = {k: np.zeros_like(v) for k, v in in_maps[REAL_CORES[0]].items()}
    for core in range(8):
        if in_maps[core] is None:
            in_maps[core] = zeros
    res = run_bass_kernel_spmd(nc, in_maps, core_ids=list(range(8)))
    out = np.stack([res.results[core]["outT"].T for core in REAL_CORES], 0)
    return np.ascontiguousarray(out.astype(np.float32))
```
